# Optimizing a Trainium2 kernel written in Bass

```python
import math
import jax, jax.numpy as jnp
from jax import lax
import numpy as np


D_MODEL = 1024
BATCH = 1
SEQ = 16384
DEPTH = 4

GRID_W = 64
ROPE_THETA = 10000.0
EPS = 1e-6
NEG_INF = -1e30
Q_BLOCK = 128

MLA_HEADS = 4
MLA_Q_RANK = 192
MLA_KV_RANK = 128
MLA_NOPE_DIM = 64
MLA_ROPE_DIM = 32
MLA_V_DIM = 64
DIFF_HEADS = 4
DIFF_QK_DIM = 32
DIFF_V_DIM = 64
SWA_Q_HEADS = 4
SWA_KV_HEADS = 2
SWA_GROUP = SWA_Q_HEADS // SWA_KV_HEADS
SWA_HEAD_DIM = 64
WINDOW = 128
SWA_BLOCK = 128
NA_HEADS = 4
NA_HEAD_DIM = 64
NA_ROWS_MAX = 8
NA_COLS = 16
D_FF = 4 * D_MODEL

A_OUT = MLA_HEADS * MLA_V_DIM
B_OUT = DIFF_HEADS * DIFF_V_DIM
C_OUT = SWA_Q_HEADS * SWA_HEAD_DIM
D_OUT = NA_HEADS * NA_HEAD_DIM
MIX_WIDTH = A_OUT + B_OUT + C_OUT + D_OUT

IN_SPLIT_WIDTHS = (
    MLA_Q_RANK, MLA_KV_RANK, MLA_ROPE_DIM,
    DIFF_HEADS * 2 * DIFF_QK_DIM, DIFF_HEADS * 2 * DIFF_QK_DIM, B_OUT,
    SWA_Q_HEADS * SWA_HEAD_DIM, SWA_KV_HEADS * SWA_HEAD_DIM, SWA_KV_HEADS * SWA_HEAD_DIM,
    NA_HEADS * NA_HEAD_DIM, NA_HEADS * NA_HEAD_DIM, NA_HEADS * NA_HEAD_DIM,
)
IN_COLS = sum(IN_SPLIT_WIDTHS)

kernel_name = 'hybrid_parallel_head_group_encoder'


def _split_points():
    pts, acc = [], 0
    for w in IN_SPLIT_WIDTHS[:-1]:
        acc += w
        pts.append(acc)
    return pts


def rms_norm(x, g):
    xf = x.astype(jnp.float32)
    y = xf * lax.rsqrt(jnp.mean(xf * xf, axis=-1, keepdims=True) + EPS)
    return (y * g.astype(jnp.float32)).astype(x.dtype)


def rope_table(seq_len, dim):
    inv = 1.0 / (ROPE_THETA ** (jnp.arange(0, dim, 2, dtype=jnp.float32) / dim))
    ang = jnp.arange(seq_len, dtype=jnp.float32)[:, None] * inv[None, :]
    return jnp.cos(ang), jnp.sin(ang)


def apply_rope(x, cos, sin):
    shp = (1, x.shape[1]) + (1,) * (x.ndim - 3) + (cos.shape[-1],)
    c, s = cos.reshape(shp), sin.reshape(shp)
    xf = x.astype(jnp.float32)
    half = x.shape[-1] // 2
    x1, x2 = xf[..., :half], xf[..., half:]
    return jnp.concatenate([x1 * c - x2 * s, x2 * c + x1 * s], axis=-1).astype(x.dtype)


def dense_block_attention(q, k, v, scale):
    B, S, H, dq = q.shape
    nb = S // Q_BLOCK
    qb = jnp.moveaxis(q.reshape(B, nb, Q_BLOCK, H, dq), 1, 0)

    def one_block(qi):
        s = jnp.einsum('bqhd,bkhd->bhqk', qi, k, preferred_element_type=jnp.float32) * scale
        p = jax.nn.softmax(s, axis=-1)
        return jnp.einsum('bhqk,bkhd->bqhd', p.astype(v.dtype), v)

    o = lax.map(one_block, qb)
    return jnp.moveaxis(o, 0, 1).reshape(B, S, H, v.shape[-1])


def mla_mixer(cq, ckv, kr, q_norm_g, w_uq, kv_norm_g, w_uk, w_uv, cos, sin):
    B, S, _ = cq.shape
    q = (rms_norm(cq, q_norm_g) @ w_uq).reshape(B, S, MLA_HEADS, MLA_NOPE_DIM + MLA_ROPE_DIM)
    q = jnp.concatenate([q[..., :MLA_NOPE_DIM], apply_rope(q[..., MLA_NOPE_DIM:], cos, sin)], axis=-1)
    c = rms_norm(ckv, kv_norm_g)
    k_nope = (c @ w_uk).reshape(B, S, MLA_HEADS, MLA_NOPE_DIM)
    v = (c @ w_uv).reshape(B, S, MLA_HEADS, MLA_V_DIM)
    k_pe = apply_rope(kr[:, :, None, :], cos, sin)
    k = jnp.concatenate([k_nope, jnp.broadcast_to(k_pe, (B, S, MLA_HEADS, MLA_ROPE_DIM))], axis=-1)
    o = dense_block_attention(q, k, v, (MLA_NOPE_DIM + MLA_ROPE_DIM) ** -0.5)
    return o.reshape(B, S, A_OUT)


def diff_mixer(q, k, v, lq1, lk1, lq2, lk2, subln_g, lambda_init, cos, sin):
    B, S, _ = q.shape
    q = apply_rope(q.reshape(B, S, DIFF_HEADS, 2, DIFF_QK_DIM), cos, sin)
    k = apply_rope(k.reshape(B, S, DIFF_HEADS, 2, DIFF_QK_DIM), cos, sin)
    v = v.reshape(B, S, DIFF_HEADS, DIFF_V_DIM)
    f32 = jnp.float32
    lam = (jnp.exp(jnp.sum(lq1.astype(f32) * lk1.astype(f32)))
           - jnp.exp(jnp.sum(lq2.astype(f32) * lk2.astype(f32))) + lambda_init)
    scale = DIFF_QK_DIM ** -0.5
    nb = S // Q_BLOCK
    qb = jnp.moveaxis(q.reshape(B, nb, Q_BLOCK, DIFF_HEADS, 2, DIFF_QK_DIM), 1, 0)

    def one_block(qi):
        s = jnp.einsum('bqhcd,bkhcd->bhcqk', qi, k, preferred_element_type=f32) * scale
        p = jax.nn.softmax(s, axis=-1)
        w = p[:, :, 0] - lam * p[:, :, 1]
        return jnp.einsum('bhqk,bkhd->bqhd', w.astype(v.dtype), v)

    o = jnp.moveaxis(lax.map(one_block, qb), 0, 1).reshape(B, S, DIFF_HEADS, DIFF_V_DIM)
    o = rms_norm(o, subln_g) * (1.0 - lambda_init)
    return o.reshape(B, S, B_OUT)


def swa_mixer(q, k, v, sinks, cos, sin):
    B, S, _ = q.shape
    q = apply_rope(q.reshape(B, S, SWA_KV_HEADS, SWA_GROUP, SWA_HEAD_DIM), cos, sin)
    k = apply_rope(k.reshape(B, S, SWA_KV_HEADS, SWA_HEAD_DIM), cos, sin)
    v = v.reshape(B, S, SWA_KV_HEADS, SWA_HEAD_DIM)
    nb = S // SWA_BLOCK
    qb = q.reshape(B, nb, SWA_BLOCK, SWA_KV_HEADS, SWA_GROUP, SWA_HEAD_DIM)

    def band(t):
        tp = jnp.pad(t, ((0, 0), (SWA_BLOCK, SWA_BLOCK), (0, 0), (0, 0)))
        tp = tp.reshape(B, nb + 2, SWA_BLOCK, SWA_KV_HEADS, SWA_HEAD_DIM)
        return jnp.concatenate([tp[:, :-2], tp[:, 1:-1], tp[:, 2:]], axis=2)

    kb, vb = band(k), band(v)
    s = jnp.einsum('bnqkgd,bnjkd->bnkgqj', qb, kb,
                   preferred_element_type=jnp.float32) * (SWA_HEAD_DIM ** -0.5)
    qi = jnp.arange(SWA_BLOCK)[:, None]
    kj = jnp.arange(3 * SWA_BLOCK)[None, :]
    rel = kj - SWA_BLOCK - qi
    kabs = jnp.arange(nb)[:, None] * SWA_BLOCK - SWA_BLOCK + jnp.arange(3 * SWA_BLOCK)[None, :]
    valid = (jnp.abs(rel) <= WINDOW)[None] & ((kabs >= 0) & (kabs < S))[:, None, :]
    s = jnp.where(valid[None, :, None, None], s, NEG_INF)
    sink = jnp.broadcast_to(sinks.astype(jnp.float32).reshape(1, 1, SWA_KV_HEADS, SWA_GROUP, 1, 1),
                            s.shape[:-1] + (1,))
    p = jax.nn.softmax(jnp.concatenate([s, sink], axis=-1), axis=-1)[..., :-1]
    o = jnp.einsum('bnkgqj,bnjkd->bnqkgd', p.astype(v.dtype), vb)
    return o.reshape(B, S, C_OUT)


def na_mixer(q, k, v, rpb):
    B, S, _ = q.shape
    rows = S // GRID_W
    kh = min(NA_ROWS_MAX, rows)
    shp = (B, rows, GRID_W, NA_HEADS, NA_HEAD_DIM)
    q, k, v = q.reshape(shp), k.reshape(shp), v.reshape(shp)
    row_start = jnp.clip(jnp.arange(rows) - kh // 2, 0, rows - kh)
    col = np.arange(GRID_W)
    col_start = np.clip(col - NA_COLS // 2, 0, GRID_W - NA_COLS)
    col_idx = col_start[:, None] + np.arange(NA_COLS)[None, :]
    col_off = col_idx - col[:, None]
    rpb_cols = rpb[:, :, col_off + NA_COLS - 1]
    scale = NA_HEAD_DIM ** -0.5

    def one_row(args):
        q_r, start, r = args
        k_r = lax.dynamic_slice_in_dim(k, start, kh, axis=1)[:, :, col_idx]
        v_r = lax.dynamic_slice_in_dim(v, start, kh, axis=1)[:, :, col_idx]
        bias = jnp.take(rpb_cols, start + jnp.arange(kh) - r + NA_ROWS_MAX - 1, axis=1)
        s = jnp.einsum('bwhd,bawchd->bhwac', q_r, k_r, preferred_element_type=jnp.float32) * scale
        s = s + jnp.transpose(bias, (0, 2, 1, 3)).astype(jnp.float32)[None]
        p = jax.nn.softmax(s.reshape(B, NA_HEADS, GRID_W, kh * NA_COLS), axis=-1).reshape(s.shape)
        return jnp.einsum('bhwac,bawchd->bwhd', p.astype(v.dtype), v_r)

    o = lax.map(one_row, (jnp.moveaxis(q, 1, 0), row_start, jnp.arange(rows)))
    return jnp.moveaxis(o, 0, 1).reshape(B, S, D_OUT)


def setup_inputs(seed: int = 0) -> dict:
    key = jax.random.key(seed)
    ks = jax.random.split(key, 24)
    f32 = jnp.float32
    L = DEPTH

    def nrm(k, shape, scale):
        return jax.random.normal(k, shape, f32) * scale

    def gain(k, shape):
        return 1.0 + 0.05 * jax.random.normal(k, shape, f32)

    return {
        'x': nrm(ks[0], (BATCH, SEQ, D_MODEL), 1.0),
        'norm1_g': gain(ks[1], (L, D_MODEL)),
        'w_in': nrm(ks[2], (L, D_MODEL, IN_COLS), D_MODEL ** -0.5),
        'mla_q_norm_g': gain(ks[3], (L, MLA_Q_RANK)),
        'mla_w_uq': nrm(ks[4], (L, MLA_Q_RANK, MLA_HEADS * (MLA_NOPE_DIM + MLA_ROPE_DIM)), MLA_Q_RANK ** -0.5),
        'mla_kv_norm_g': gain(ks[5], (L, MLA_KV_RANK)),
        'mla_w_uk': nrm(ks[6], (L, MLA_KV_RANK, MLA_HEADS * MLA_NOPE_DIM), MLA_KV_RANK ** -0.5),
        'mla_w_uv': nrm(ks[7], (L, MLA_KV_RANK, MLA_HEADS * MLA_V_DIM), MLA_KV_RANK ** -0.5),
        'diff_lambda_q1': nrm(ks[8], (L, DIFF_QK_DIM), 0.1),
        'diff_lambda_k1': nrm(ks[9], (L, DIFF_QK_DIM), 0.1),
        'diff_lambda_q2': nrm(ks[10], (L, DIFF_QK_DIM), 0.1),
        'diff_lambda_k2': nrm(ks[11], (L, DIFF_QK_DIM), 0.1),
        'diff_subln_g': gain(ks[12], (L, DIFF_V_DIM)),
        'swa_sinks': nrm(ks[13], (L, SWA_Q_HEADS), 0.5),
        'na_rpb': nrm(ks[14], (L, NA_HEADS, 2 * NA_ROWS_MAX - 1, 2 * NA_COLS - 1), 0.1),
        'out_g_mla': gain(ks[15], (L, A_OUT)),
        'out_g_swa': gain(ks[16], (L, C_OUT)),
        'out_g_na': gain(ks[17], (L, D_OUT)),
        'w_out': nrm(ks[18], (L, MIX_WIDTH, D_MODEL), MIX_WIDTH ** -0.5),
        'norm2_g': gain(ks[19], (L, D_MODEL)),
        'w_up': nrm(ks[20], (L, D_MODEL, D_FF), D_MODEL ** -0.5),
        'w_down': nrm(ks[21], (L, D_FF, D_MODEL), D_FF ** -0.5),
        'final_norm_g': gain(ks[22], (D_MODEL,)),
    }


def reference(x, norm1_g, w_in, mla_q_norm_g, mla_w_uq, mla_kv_norm_g, mla_w_uk, mla_w_uv,
              diff_lambda_q1, diff_lambda_k1, diff_lambda_q2, diff_lambda_k2, diff_subln_g,
              swa_sinks, na_rpb, out_g_mla, out_g_swa, out_g_na, w_out, norm2_g, w_up, w_down,
              final_norm_g):
    S = x.shape[1]
    cos32, sin32 = rope_table(S, MLA_ROPE_DIM)
    cos64, sin64 = rope_table(S, SWA_HEAD_DIM)
    splits = _split_points()
    for l in range(DEPTH):
        h = rms_norm(x, norm1_g[l])
        proj = h @ w_in[l]
        (a_cq, a_ckv, a_kr, b_q, b_k, b_v, c_q, c_k, c_v, d_q, d_k, d_v) = jnp.split(proj, splits, axis=-1)
        y_a = mla_mixer(a_cq, a_ckv, a_kr, mla_q_norm_g[l], mla_w_uq[l], mla_kv_norm_g[l],
                        mla_w_uk[l], mla_w_uv[l], cos32, sin32)
        lambda_init = 0.8 - 0.6 * math.exp(-0.3 * l)
        y_b = diff_mixer(b_q, b_k, b_v, diff_lambda_q1[l], diff_lambda_k1[l], diff_lambda_q2[l],
                         diff_lambda_k2[l], diff_subln_g[l], lambda_init, cos32, sin32)
        y_c = swa_mixer(c_q, c_k, c_v, swa_sinks[l], cos64, sin64)
        y_d = na_mixer(d_q, d_k, d_v, na_rpb[l])
        mix = jnp.concatenate([rms_norm(y_a, out_g_mla[l]), y_b,
                               rms_norm(y_c, out_g_swa[l]), rms_norm(y_d, out_g_na[l])], axis=-1)
        x = x + mix @ w_out[l]
        h = rms_norm(x, norm2_g[l])
        x = x + jnp.square(jax.nn.relu(h @ w_up[l])) @ w_down[l]
    return rms_norm(x, final_norm_g)
```

```python
import math
from contextlib import ExitStack
import numpy as np
import concourse.bass as bass
import concourse.mybir as mybir
from concourse.bass_utils import run_bass_kernel_spmd

F32 = mybir.dt.float32
BF16 = mybir.dt.bfloat16
AF = mybir.ActivationFunctionType
ALU = mybir.AluOpType

NCORES = 8
L = 4
LAMBDA_INIT_C = [0.8 - 0.6 * math.exp(-0.3 * l) for l in range(4)]
D = 1024
S = 16384
T = 2048
NTC = 4
DFF = 4096
EPS = 1e-6
NEG = -30000.0
KROWS = 928
VROWS = 910
KVROWS = KROWS + VROWS
QROWS = 1152
NSLOT = 20
NFM = 2688
NTM = 640


def _sw(n):
    h = n // 2
    return np.concatenate([np.arange(h, n), np.arange(0, h)])


def _fm_cols():
    o_cq, o_ckv, o_kr, o_bq, o_bk, o_bv, o_cq2, o_ck, o_cv, o_dq, o_dk, o_dv = (
        0, 192, 320, 352, 608, 864, 1120, 1376, 1504, 1632, 1888, 2144)
    sw32, sw64 = _sw(32), _sw(64)
    cols = []
    cols += list(range(o_cq, o_cq + 192))
    cols += list(range(o_ckv, o_ckv + 128))
    cols += list(range(o_kr, o_kr + 32))
    cols += list(o_kr + sw32)
    bq = [o_bq + m * 32 + i for m in range(8) for i in range(32)]
    bqs = [o_bq + m * 32 + i for m in range(8) for i in sw32]
    bk = [o_bk + m * 32 + i for m in range(8) for i in range(32)]
    bks = [o_bk + m * 32 + i for m in range(8) for i in sw32]
    cq = [o_cq2 + h * 64 + i for h in range(4) for i in range(64)]
    cqs = [o_cq2 + h * 64 + i for h in range(4) for i in sw64]
    ck = [o_ck + h * 64 + i for h in range(2) for i in range(64)]
    cks = [o_ck + h * 64 + i for h in range(2) for i in sw64]
    cols += bq + bqs + bk + bks + cq + cqs + ck + cks
    cols += list(range(o_dq, o_dq + 256)) + list(range(o_dk, o_dk + 256))
    tm = list(range(o_bv, o_bv + 256)) + list(range(o_cv, o_cv + 128)) + list(range(o_dv, o_dv + 256))
    return np.array(cols), np.array(tm)


def _rope_tables(rank):
    pos = (rank * T + np.arange(T)).astype(np.float32)
    out = np.zeros((4, 128, T), np.float32)
    for ti, dim in ((0, 32), (2, 64)):
        inv = (1.0 / (np.float32(10000.0) ** (np.arange(0, dim, 2, dtype=np.float32) / np.float32(dim)))).astype(np.float32)
        ang = (pos[:, None] * inv[None, :]).astype(np.float32)
        c, s = np.cos(ang).astype(np.float32), np.sin(ang).astype(np.float32)
        half = dim // 2
        for p in range(128):
            i = p % dim
            out[ti, p] = c[:, i % half]
            out[ti + 1, p] = (-s[:, i % half]) if i < half else s[:, i % half]
    return out


def _c_tables(rank):
    ki = np.arange(128)[:, None]
    qi = np.arange(128)[None, :]
    base = np.zeros((128, 3, 2, 128), np.float32)
    base[:, 0] = np.where(qi <= ki, 0.0, NEG)[:, None, :]
    base[:, 2] = np.where(ki <= qi, 0.0, NEG)[:, None, :]
    t0 = base.copy()
    t15 = base.copy()
    if rank == 0:
        t0[:, 0] = NEG
    if rank == NCORES - 1:
        t15[:, 2] = NEG
    return np.stack([base, t0, t15]).reshape(3, 128, 768)


def _d_table(rpb_l, r0):
    ki = np.arange(128)
    kr, kc = ki // 64, ki % 64
    qi = np.arange(128)
    qr, qc = qi // 64, qi % 64
    out = np.full((128, 4, 7, 128), NEG, np.float32)
    rq = r0 + qr
    rs = np.clip(rq - 4, 0, 256 - 8)
    cs = np.clip(qc - 8, 0, 64 - 16)
    for j in range(7):
        krow = r0 - 6 + 2 * j + kr
        vr = (krow[:, None] >= rs[None, :]) & (krow[:, None] < rs[None, :] + 8) & (krow[:, None] >= 0) & (krow[:, None] < 256)
        vc = (kc[:, None] >= cs[None, :]) & (kc[:, None] < cs[None, :] + 16)
        valid = vr & vc
        dr = np.clip(krow[:, None] - rq[None, :] + 7, 0, 14)
        dc = np.clip(kc[:, None] - qc[None, :] + 15, 0, 30)
        for h in range(4):
            out[:, h, j, :] = np.where(valid, rpb_l[h][dr, dc], NEG)
    return out.reshape(128, 4 * 7 * 128)


def _d_tables(rpb, rank, layers):
    res = []
    for l in layers:
        cls = []
        for ti in (5, 0, 1, 14, 15):
            r0 = rank * 32 + 2 * ti
            cls.append(_d_table(rpb[l], r0))
        res.append(np.stack(cls))
    return np.stack(res)


P_G1, P_G2, P_GQ, P_GKV, P_GA, P_GC, P_GD, P_GSUB, P_LAM, P_SINK = 0, 8, 16, 18, 19, 21, 23, 25, 26, 154
NPAR_L = 160
P_LI, P_OML = 158, 159
P_GF = 0


def _params(inp, layers):
    out = np.zeros((len(layers), 128, NPAR_L), np.float32)
    for i, l in enumerate(layers):
        out[i, :, P_G1:P_G1 + 8] = inp['norm1_g'][l].reshape(8, 128).T
        out[i, :, P_G2:P_G2 + 8] = inp['norm2_g'][l].reshape(8, 128).T
        gq = np.zeros(256, np.float32)
        gq[:192] = inp['mla_q_norm_g'][l]
        out[i, :, P_GQ:P_GQ + 2] = gq.reshape(2, 128).T
        out[i, :, P_GKV] = inp['mla_kv_norm_g'][l]
        out[i, :, P_GA:P_GA + 2] = inp['out_g_mla'][l].reshape(2, 128).T
        out[i, :, P_GC:P_GC + 2] = inp['out_g_swa'][l].reshape(2, 128).T
        out[i, :, P_GD:P_GD + 2] = inp['out_g_na'][l].reshape(2, 128).T
        out[i, :, P_GSUB] = np.tile(inp['diff_subln_g'][l], 2)
        lam = np.stack([inp['diff_lambda_q1'][l], inp['diff_lambda_k1'][l],
                        inp['diff_lambda_q2'][l], inp['diff_lambda_k2'][l]]).reshape(-1)
        out[i, :, P_LAM:P_LAM + 128] = lam[None, :]
        out[i, :, P_SINK:P_SINK + 4] = inp['swa_sinks'][l][None, :]
        out[i, :, P_LI] = LAMBDA_INIT_C[l]
        out[i, :, P_OML] = 1.0 - LAMBDA_INIT_C[l]
    return out


def _prep_shared(inp, layers):
    fm, tm = _fm_cols()
    w_in = inp['w_in']
    d = {}
    d['w_fm'] = np.ascontiguousarray(np.stack([w_in[l][:, fm] for l in layers]))
    d['w_tm'] = np.ascontiguousarray(np.stack([w_in[l][:, tm] for l in layers]))
    uqs_cols = np.concatenate([np.concatenate([h * 96 + np.arange(64), h * 96 + 64 + _sw(32)]) for h in range(4)])
    d['w_uq'] = np.ascontiguousarray(np.stack([inp['mla_w_uq'][l] for l in layers]))
    d['w_uqs'] = np.ascontiguousarray(np.stack([inp['mla_w_uq'][l][:, uqs_cols] for l in layers]))
    d['w_uk'] = np.ascontiguousarray(np.stack([inp['mla_w_uk'][l] for l in layers]))
    d['w_uv'] = np.ascontiguousarray(np.stack([inp['mla_w_uv'][l] for l in layers]))
    d['w_out'] = np.ascontiguousarray(np.stack([inp['w_out'][l] for l in layers]))
    d['w_up'] = np.ascontiguousarray(np.stack([inp['w_up'][l] for l in layers]))
    d['w_down'] = np.ascontiguousarray(np.stack([inp['w_down'][l] for l in layers]))
    d['params'] = _params(inp, layers)
    d['gfin'] = np.ascontiguousarray(inp['final_norm_g'].reshape(8, 128).T)
    return d


def _prep_core(inp, rank, layers):
    d = {}
    d['rope'] = _rope_tables(rank)
    d['ctab'] = _c_tables(rank)
    d['dtab'] = _d_tables(inp['na_rpb'], rank, layers)
    sel = np.zeros((128, 16), np.float32)
    if rank > 0:
        sel[:, rank - 1] = 1.0
    if rank < NCORES - 1:
        sel[:, 8 + rank + 1] = 1.0
    d['sel'] = sel
    return d


ENGS = ('pe', 'act', 'dve', 'pool', 'sp')


class Buf:
    def __init__(self, name):
        self.name = name
        self.wev = {}
        self.readers = []
        self.war = []


class SemGroup:
    def __init__(self, K, name):
        self.K = K
        self.name = name
        self.sem = None
        self.cnt = 0

    def get(self):
        if self.sem is None:
            self.sem = self.K.alloc_sem()
            self.cnt = self.K.sem_base[id(self.sem)]
        return self.sem


class Op:
    __slots__ = ('eng', 'fn', 'deps', 'signal', 'ticket', 'dma', 'sg', 'cnt', 'phase')

    def __init__(self, eng, fn):
        self.eng, self.fn = eng, fn
        self.deps = []
        self.signal = False
        self.ticket = None
        self.dma = False
        self.sg = None
        self.cnt = 0


class Kern:
    def __init__(self, nc, stack):
        self.nc = nc
        self.stack = stack
        self.csem = {e: stack.enter_context(nc.semaphore('c_' + e)) for e in ('pe', 'act', 'dve', 'pool')}
        self.ccnt = {e: 0 for e in self.csem}
        self.free_sems = []
        self.sem_base = {}
        self.nsem = 0
        self.all_groups = []

    def alloc_sem(self):
        if self.free_sems:
            return self.free_sems.pop()
        s = self.stack.enter_context(self.nc.semaphore('d%d' % self.nsem))
        self.nsem += 1
        self.sem_base[id(s)] = 0
        return s


class Phase:
    def __init__(self, K, name):
        self.K = K
        self.nc = K.nc
        self.name = name
        self.ops = []
        self.groups = []

    def group(self, name):
        g = SemGroup(self.K, name)
        g.phase = self
        self.groups.append(g)
        return g

    def _wevents(self, b):
        ev = []
        if 'c' in b.wev:
            ev.append(('c', b.wev['c']))
        if 'd' in b.wev:
            ev.append(('d',) + b.wev['d'])
        return ev

    def op(self, eng, fn, reads=(), writes=(), acc=False):
        o = Op(eng, fn)
        for b in reads:
            o.deps += self._wevents(b)
        for b in writes:
            for ev in self._wevents(b):
                if acc and ev[0] == 'c' and ev[1].eng == 'pe' and eng == 'pe':
                    continue
                o.deps.append(ev)
            o.deps += b.readers + b.war
        for b in reads:
            b.readers.append(('c', o))
        for b in writes:
            if b.readers:
                b.war = b.readers
                b.readers = []
            b.wev = {'c': o}
        self.ops.append(o)
        return o

    def dma(self, q, out, in_, sg, reads=(), writes=()):
        def fn(e, out=out, in_=in_):
            return e.dma_start(out=out, in_=in_)
        o = Op(q, fn)
        o.dma = True
        for b in reads:
            o.deps += self._wevents(b)
        for b in writes:
            if 'c' in b.wev:
                o.deps.append(('c', b.wev['c']))
            o.deps += b.readers + b.war
        sg.get()
        sg.cnt += 16
        o.sg, o.cnt = sg, sg.cnt
        ev = ('d', sg, sg.cnt)
        for b in reads:
            b.readers.append(ev)
        for b in writes:
            if b.readers:
                b.war = b.readers
                b.readers = []
            b.wev['d'] = (sg, sg.cnt)
        self.ops.append(o)
        return o

    def c(self, eng, method, reads=(), writes=(), acc=False, **kw):
        def fn(e, method=method, kw=kw):
            return getattr(e, method)(**kw)
        return self.op(eng, fn, reads, writes, acc)

    def emit(self):
        K = self.K
        nc = self.nc
        for o in self.ops:
            o.phase = self
        for o in self.ops:
            o.deps = [ev for ev in o.deps if ev[1].phase is self]
            for ev in o.deps:
                if ev[0] == 'c':
                    ev[1].signal = True
        for o in self.ops:
            if not o.dma and o.signal:
                K.ccnt[o.eng] += 1
                o.ticket = K.ccnt[o.eng]
        per = {e: [] for e in ENGS}
        for o in self.ops:
            per[o.eng].append(o)
        final_waits = [(g.sem, g.cnt) for g in self.groups if g.sem is not None]

        def run(ename, e):
            seen = {}
            for o in per[ename]:
                need = {}
                for ev in o.deps:
                    if ev[0] == 'c':
                        p = ev[1]
                        if p.eng == 'pe' and ename == 'pe':
                            continue
                        sem, val = K.csem[p.eng], p.ticket
                    else:
                        sem, val = ev[1].sem, ev[2]
                    k = id(sem)
                    if seen.get(k, -1) >= val:
                        continue
                    if k not in need or need[k][1] < val:
                        need[k] = (sem, val)
                for k, (sem, val) in need.items():
                    e.wait_ge(sem, val)
                    seen[k] = val
                inst = o.fn(e)
                if o.dma:
                    inst.then_inc(o.sg.sem, 16)
                elif o.signal:
                    inst.then_inc(K.csem[o.eng], 1)
            if ename == 'sp':
                for sem, val in final_waits:
                    e.wait_ge(sem, val)
                for en in ('pe', 'act', 'dve', 'pool'):
                    if K.ccnt[en] > 0:
                        e.wait_ge(K.csem[en], K.ccnt[en])

        with nc.Block() as block:
            @block.sync
            def _(e):
                run('sp', e)

            @block.tensor
            def _(e):
                run('pe', e)

            @block.scalar
            def _(e):
                run('act', e)

            @block.vector
            def _(e):
                run('dve', e)

            @block.gpsimd
            def _(e):
                run('pool', e)
        for g in self.groups:
            if g.sem is not None:
                K.sem_base[id(g.sem)] = g.cnt
                K.free_sems.append(g.sem)
                g.sem = None
        self.ops = []


FM_GROUPS = [
    ('cq0', 0, 128), ('cq1', 128, 64), ('ckv', 192, 128), ('kr', 320, 32), ('krs', 352, 32),
    ('bq0', 384, 128), ('bq1', 512, 128), ('bqs0', 640, 128), ('bqs1', 768, 128),
    ('bk0', 896, 128), ('bk1', 1024, 128), ('bks0', 1152, 128), ('bks1', 1280, 128),
    ('cq0_', 1408, 128), ('cq1_', 1536, 128), ('cqs0', 1664, 128), ('cqs1', 1792, 128),
    ('ck', 1920, 128), ('cks', 2048, 128),
    ('dq0', 2176, 128), ('dq1', 2304, 128), ('dk0', 2432, 128), ('dk1', 2560, 128),
]
FM_OFF = {n: (o, m) for n, o, m in FM_GROUPS}
ST_KB, ST_KC, ST_KD, ST_QB, ST_QC, ST_QD = 0, 2, 3, 5, 7, 9
LAMBDA_INIT = [0.8 - 0.6 * math.exp(-0.3 * l) for l in range(L)]


def flat_ps(psum, b0, nb):
    return psum[:, b0:b0 + nb, :].rearrange("p a b -> p (a b)")


class Builder:
    def __init__(self, nc, stack):
        self.nc = nc
        self.stack = stack
        self.K = Kern(nc, stack)

    def sb(self, name, shape, dtype):
        self.uid = getattr(self, 'uid', 0) + 1
        return self.cur.enter_context(self.nc.sbuf_tensor('s%d_%s' % (self.uid, name), list(shape), dtype))

    def setup_persistent(self):
        nc, st = self.nc, self.stack
        self.xT = st.enter_context(nc.sbuf_tensor('xT', [128, 8, T], F32))
        self.xT_b = [[Buf('xT_%d_%d' % (c, t)) for t in range(NTC)] for c in range(8)]
        self.par = st.enter_context(nc.sbuf_tensor('par', [128, NPAR_L], F32))
        self.par_b = Buf('par')
        self.cst = st.enter_context(nc.sbuf_tensor('cst', [128, 4], F32))
        self.ones_f = st.enter_context(nc.sbuf_tensor('ones_f', [128, 128], F32))
        self.bd_f = st.enter_context(nc.sbuf_tensor('bd_f', [128, 128], F32))
        self.cst_b = Buf('cst')
        self.psum = st.enter_context(nc.psum_tensor('psum', [128, 8, 512], F32))
        self.ps_b = [Buf('ps%d' % i) for i in range(8)]

    def emit_init(self, xT_dram):
        P = Phase(self.K, 'init')
        g = P.group('xload')
        for c in range(8):
            P.dma('sp', self.xT[:, c, :], xT_dram[c * 128:(c + 1) * 128, :], g, writes=list(self.xT_b[c]))
        P.c('dve', 'memset', writes=[self.cst_b], ap=self.cst[:, 0:1], constant=EPS)
        P.c('dve', 'memset', writes=[self.cst_b], ap=self.cst[:, 1:2], constant=1.0)
        P.c('dve', 'memset', writes=[self.cst_b], ap=self.ones_f[:], constant=1.0)
        P.c('dve', 'memset', writes=[self.cst_b], ap=self.bd_f[:], constant=0.0)
        P.c('dve', 'memset', writes=[self.cst_b], ap=self.bd_f[0:64, 0:64], constant=1.0)
        P.c('dve', 'memset', writes=[self.cst_b], ap=self.bd_f[64:128, 64:128], constant=1.0)
        P.emit()

    def load_params(self, P, W, li):
        gp = P.group('par')
        P.dma('sp', self.par[:], W['params'][li], gp, writes=[self.par_b])

    def rstd_from(self, P, srcs, n, ps_i, rstd_ap, rstd_b, sq_tiles, sq_bufs, lhs=None):
        ps = self.psum[:, ps_i, :]
        lhs_t = self.ones_f if lhs is None else lhs
        ns = len(srcs)
        for i, (ap, kk, bufs) in enumerate(srcs):
            sq, sqb = sq_tiles[i % 2], sq_bufs[i % 2]
            P.c('act', 'activation', reads=bufs, writes=[sqb], out=sq[0:kk, :], in_=ap, func=AF.Square)
            P.c('pe', 'matmul', reads=[sqb, self.cst_b], writes=[self.ps_b[ps_i]], acc=(i > 0),
                out=ps, lhsT=lhs_t[0:kk, :], rhs=sq[0:kk, :], start=(i == 0), stop=(i == ns - 1))
        tmp = sq_tiles[0]
        P.c('act', 'activation', reads=[self.ps_b[ps_i], self.cst_b], writes=[sq_bufs[0]],
            out=tmp[:], in_=ps, func=AF.Sqrt, bias=self.cst[:, 0:1], scale=1.0 / n)
        P.c('dve', 'reciprocal', reads=[sq_bufs[0]], writes=[rstd_b], out=rstd_ap, in_=tmp[:])

    def emit_phase1(self, li, W, q_loc, kv_loc, rope):
        K = self.K
        P = Phase(K, 'p1')
        psum, ps_b, xT, par = self.psum, self.ps_b, self.xT, self.par
        with ExitStack() as es:
            self.cur = es
            w_fm = self.sb('w_fm', [128, 8, NFM], BF16)
            w_tm = self.sb('w_tm', [128, 8, NTM], BF16)
            w_uq = self.sb('w_uq', [128, 2, 384], BF16)
            w_uqs = self.sb('w_uqs', [128, 2, 384], BF16)
            w_uk = self.sb('w_uk', [128, 256], BF16)
            w_uv = self.sb('w_uv', [128, 256], BF16)
            tab = self.sb('tab', [128, 4, 512], F32)
            hT = self.sb('hT', [128, 8, 512], BF16)
            sq = [self.sb('sq%d' % i, [128, 512], F32) for i in range(2)]
            rstd = self.sb('rstd', [128, 512], F32)
            cqf = self.sb('cqf', [128, 2, 512], F32)
            cqn = self.sb('cqn', [128, 2, 512], BF16)
            ckf = self.sb('ckf', [128, 512], F32)
            cn = self.sb('cn', [128, 512], BF16)
            t1 = [self.sb('t1_%d' % i, [128, 512], F32) for i in range(2)]
            t2 = [self.sb('t2_%d' % i, [128, 512], F32) for i in range(2)]
            st128 = self.sb('st128', [128, 11, 512], BF16)
            st64 = self.sb('st64', [64, 4, 512], BF16)
            st32 = self.sb('st32', [32, 512], BF16)
            st96 = self.sb('st96', [96, 4, 512], BF16)
            vst = self.sb('vst', [128, 14, 4, 65], BF16)
            b_w, b_tab, b_hT, b_rstd = Buf('w'), Buf('tab'), Buf('hT'), Buf('rstd')
            b_sq = [Buf('sq0'), Buf('sq1')]
            b_cqf, b_cqn, b_ckf, b_cn = Buf('cqf'), Buf('cqn'), Buf('ckf'), Buf('cn')
            b_t1 = [Buf('t1a'), Buf('t1b')]
            b_t2 = [Buf('t2a'), Buf('t2b')]
            b_st128 = [Buf('st128_%d' % i) for i in range(11)]
            b_st64 = [Buf('st64_%d' % i) for i in range(4)]
            b_st32 = Buf('st32')
            b_st96 = [Buf('st96_%d' % i) for i in range(4)]
            b_vst, b_dram = Buf('vst'), Buf('dram_out')

            gw = P.group('w')
            self.load_params(P, W, li)
            for k in range(8):
                P.dma('pool', w_fm[:, k, :], W['w_fm'][li, k * 128:(k + 1) * 128, :], gw, writes=[b_w])
            for k in range(8):
                P.dma('pool', w_tm[:, k, :], W['w_tm'][li, k * 128:(k + 1) * 128, :], gw, writes=[b_w])
            for wt, nm in ((w_uq, 'w_uq'), (w_uqs, 'w_uqs')):
                P.dma('pool', wt[:, 0, :], W[nm][li, 0:128, :], gw, writes=[b_w])
                P.dma('pool', wt[0:64, 1, :], W[nm][li, 128:192, :], gw, writes=[b_w])
            P.dma('pool', w_uk[:], W['w_uk'][li], gw, writes=[b_w])
            P.dma('pool', w_uv[:], W['w_uv'][li], gw, writes=[b_w])
            P.c('dve', 'memset', writes=[b_vst], ap=vst[:, :, :, 64:65], constant=1.0)
            gt = P.group('tab')
            gst = [P.group('st_k'), P.group('st_q'), P.group('st_v')]
            bank = [0]

            def nb():
                bank[0] = bank[0] % 7 + 1
                return bank[0]

            def proj(name):
                off, m = FM_OFF[name]
                pi = nb()
                ps = psum[0:m, pi, :]
                for k in range(8):
                    P.c('pe', 'matmul', reads=[b_w, b_hT], writes=[ps_b[pi]], acc=(k > 0),
                        out=ps, lhsT=w_fm[:, k, off:off + m], rhs=hT[:, k, :], start=(k == 0), stop=(k == 7))
                return pi, ps

            def rope_mix(pix, psx, pis, pss, tbl, m, out_ap, out_b, slot, p0=0):
                a, b = t1[slot], t2[slot]
                P.c('dve', 'tensor_tensor', reads=[ps_b[pix], b_tab], writes=[b_t1[slot]],
                    out=a[p0:p0 + m, :], in0=psx, in1=tab[p0:p0 + m, tbl, :], op=ALU.mult)
                P.c('dve', 'tensor_tensor', reads=[ps_b[pis], b_tab], writes=[b_t2[slot]],
                    out=b[p0:p0 + m, :], in0=pss, in1=tab[p0:p0 + m, tbl + 1, :], op=ALU.mult)
                P.c('pool', 'tensor_tensor', reads=[b_t1[slot], b_t2[slot]], writes=[out_b],
                    out=out_ap, in0=a[p0:p0 + m, :], in1=b[p0:p0 + m, :], op=ALU.add)

            def rope_pair(nx, ns, tbl, m, out_ap, out_b, slot):
                pix, psx = proj(nx)
                pis, pss = proj(ns)
                rope_mix(pix, psx, pis, pss, tbl, m, out_ap, out_b, slot)

            for tc in range(NTC):
                tsl = slice(tc * 512, (tc + 1) * 512)
                P.dma('sp', tab[:], rope[:, :, tsl].rearrange("a p t -> p a t"), gt, writes=[b_tab])
                srcs = [(xT[:, c, tsl], 128, [self.xT_b[c][tc]]) for c in range(8)]
                self.rstd_from(P, srcs, float(D), 0, rstd[:], b_rstd, sq, b_sq)
                for c in range(8):
                    P.c('dve', 'scalar_tensor_tensor', reads=[self.xT_b[c][tc], self.par_b, b_rstd], writes=[b_hT],
                        out=hT[:, c, :], in0=xT[:, c, tsl], scalar=par[:, P_G1 + c:P_G1 + c + 1], in1=rstd[:],
                        op0=ALU.mult, op1=ALU.mult)
                for i, nm in enumerate(('cq0', 'cq1')):
                    pi, ps = proj(nm)
                    m = FM_OFF[nm][1]
                    P.c('act', 'activation', reads=[ps_b[pi]], writes=[b_cqf], out=cqf[0:m, i, :], in_=ps, func=AF.Copy)
                pi, ps = proj('ckv')
                P.c('act', 'activation', reads=[ps_b[pi]], writes=[b_ckf], out=ckf[:], in_=ps, func=AF.Copy)
                rope_pair('kr', 'krs', 0, 32, st32[:], b_st32, 0)
                rope_pair('bq0', 'bqs0', 0, 128, st128[:, ST_QB, :], b_st128[ST_QB], 1)
                rope_pair('bq1', 'bqs1', 0, 128, st128[:, ST_QB + 1, :], b_st128[ST_QB + 1], 0)
                rope_pair('bk0', 'bks0', 0, 128, st128[:, ST_KB, :], b_st128[ST_KB], 1)
                rope_pair('bk1', 'bks1', 0, 128, st128[:, ST_KB + 1, :], b_st128[ST_KB + 1], 0)
                rope_pair('cq0_', 'cqs0', 2, 128, st128[:, ST_QC, :], b_st128[ST_QC], 1)
                rope_pair('cq1_', 'cqs1', 2, 128, st128[:, ST_QC + 1, :], b_st128[ST_QC + 1], 0)
                rope_pair('ck', 'cks', 2, 128, st128[:, ST_KC, :], b_st128[ST_KC], 1)
                for nm, idx in (('dq0', ST_QD), ('dq1', ST_QD + 1), ('dk0', ST_KD), ('dk1', ST_KD + 1)):
                    pi, ps = proj(nm)
                    P.c('act', 'activation', reads=[ps_b[pi]], writes=[b_st128[idx]], out=st128[:, idx, :], in_=ps, func=AF.Copy)
                srcs = [(cqf[:, 0, :], 128, [b_cqf]), (cqf[0:64, 1, :], 64, [b_cqf])]
                self.rstd_from(P, srcs, 192.0, 0, rstd[:], b_rstd, sq, b_sq)
                for i, m in ((0, 128), (1, 64)):
                    P.c('dve', 'scalar_tensor_tensor', reads=[b_cqf, self.par_b, b_rstd], writes=[b_cqn],
                        out=cqn[0:m, i, :], in0=cqf[0:m, i, :], scalar=par[0:m, P_GQ + i:P_GQ + i + 1], in1=rstd[0:m, :],
                        op0=ALU.mult, op1=ALU.mult)
                self.rstd_from(P, [(ckf[:], 128, [b_ckf])], 128.0, 0, rstd[:], b_rstd, sq, b_sq)
                P.c('dve', 'scalar_tensor_tensor', reads=[b_ckf, self.par_b, b_rstd], writes=[b_cn],
                    out=cn[:], in0=ckf[:], scalar=par[:, P_GKV:P_GKV + 1], in1=rstd[:], op0=ALU.mult, op1=ALU.mult)
                for h in range(4):
                    res = []
                    for wt in (w_uq, w_uqs):
                        pi = nb()
                        ps = psum[0:96, pi, :]
                        for i, m in ((0, 128), (1, 64)):
                            P.c('pe', 'matmul', reads=[b_w, b_cqn], writes=[ps_b[pi]], acc=(i > 0),
                                out=ps, lhsT=wt[0:m, i, h * 96:(h + 1) * 96], rhs=cqn[0:m, i, :], start=(i == 0), stop=(i == 1))
                        res.append((pi, ps))
                    P.c('act', 'activation', reads=[ps_b[res[0][0]]], writes=[b_st96[h]],
                        out=st96[0:64, h, :], in_=res[0][1][0:64, :], func=AF.Copy)
                    rope_mix(res[0][0], res[0][1][64:96, :], res[1][0], res[1][1][64:96, :], 0, 32,
                             st96[64:96, h, :], b_st96[h], h % 2, p0=64)
                for h in range(4):
                    pi = nb()
                    ps = psum[0:64, pi, :]
                    P.c('pe', 'matmul', reads=[b_w, b_cn], writes=[ps_b[pi]],
                        out=ps, lhsT=w_uk[:, h * 64:(h + 1) * 64], rhs=cn[:], start=True, stop=True)
                    P.c('act', 'activation', reads=[ps_b[pi]], writes=[b_st64[h]], out=st64[:, h, :], in_=ps, func=AF.Copy)
                for tt in range(4):
                    ts2 = slice(tt * 128, (tt + 1) * 128)
                    pi = nb()
                    ps = psum[:, pi, 0:256]
                    P.c('pe', 'matmul', reads=[b_w, b_cn], writes=[ps_b[pi]], out=ps, lhsT=cn[:, ts2], rhs=w_uv[:], start=True, stop=True)
                    P.c('dve', 'tensor_copy', reads=[ps_b[pi]], writes=[b_vst],
                        out=vst[:, 0:4, tt, 0:64], in_=ps.rearrange("p (s e) -> p s e", e=64))
                    pi1 = nb()
                    ps1 = psum[:, pi1, :]
                    for k in range(8):
                        P.c('pe', 'matmul', reads=[b_w, b_hT], writes=[ps_b[pi1]], acc=(k > 0),
                            out=ps1, lhsT=hT[:, k, ts2], rhs=w_tm[:, k, 0:512], start=(k == 0), stop=(k == 7))
                    P.c('act', 'activation', reads=[ps_b[pi1]], writes=[b_vst],
                        out=vst[:, 4:12, tt, 0:64], in_=ps1.rearrange("p (s e) -> p s e", e=64), func=AF.Copy)
                    pi2 = nb()
                    ps2 = psum[:, pi2, 0:128]
                    for k in range(8):
                        P.c('pe', 'matmul', reads=[b_w, b_hT], writes=[ps_b[pi2]], acc=(k > 0),
                            out=ps2, lhsT=hT[:, k, ts2], rhs=w_tm[:, k, 512:640], start=(k == 0), stop=(k == 7))
                    P.c('dve', 'tensor_copy', reads=[ps_b[pi2]], writes=[b_vst],
                        out=vst[:, 12:14, tt, 0:64], in_=ps2.rearrange("p (s e) -> p s e", e=64))
                P.dma('sp', kv_loc[0:256, tsl].rearrange("(h p) t -> p h t", p=64), st64[:], gst[0], reads=b_st64, writes=[b_dram])
                P.dma('sp', kv_loc[256:288, tsl], st32[:], gst[0], reads=[b_st32], writes=[b_dram])
                P.dma('sp', kv_loc[288:928, tsl].rearrange("(g p) t -> p g t", p=128), st128[:, 0:5, :], gst[0],
                      reads=b_st128[0:5], writes=[b_dram])
                P.dma('sp', q_loc[0:384, tsl].rearrange("(h p) t -> p h t", p=96), st96[:], gst[1], reads=b_st96, writes=[b_dram])
                P.dma('sp', q_loc[384:1152, tsl].rearrange("(g p) t -> p g t", p=128), st128[:, 5:11, :], gst[1],
                      reads=b_st128[5:11], writes=[b_dram])
                vdst = bass.AP(kv_loc.tensor, kv_loc.offset + KROWS * T + tc * 260, [[1040, 128], [65 * T, 14], [1, 260]])
                P.dma('sp', vdst, vst[:].rearrange("p s b e -> p s (b e)"), gst[2], reads=[b_vst], writes=[b_dram])
            P.emit()

    def emit_dense(self, q_loc, kv_all, mixu, rden, passes=None):
        K = self.K
        P = Phase(K, 'dense')
        psum, ps_b = self.psum, self.ps_b
        if passes is None:
            passes = []
            for h in range(4):
                passes.append(dict(slot=h, d=96, q0=h * 96, krows=[(h * 64, 64), (256, 32)], vs=h, scale=96 ** -0.5))
            for m in range(8):
                h, c = m // 2, m % 2
                passes.append(dict(slot=4 + c * 4 + h, d=32, q0=384 + m * 32, krows=[(288 + m * 32, 32)], vs=4 + h, scale=32 ** -0.5))
        with ExitStack() as es:
            self.cur = es
            qT = [self.sb('qT%d' % i, [128, T], BF16) for i in range(2)]
            kc = [self.sb('kc%d' % i, [96, T], BF16) for i in range(2)]
            vc = [self.sb('vc%d' % i, [128, 16 * 65], BF16) for i in range(2)]
            pT = [self.sb('pT%d' % i, [128, 1024], BF16) for i in range(4)]
            osb = [self.sb('osb%d' % i, [65, T], F32) for i in range(2)]
            b_q = [Buf('q0'), Buf('q1')]
            b_k = [Buf('k0'), Buf('k1')]
            b_v = [Buf('v0'), Buf('v1')]
            b_p = [Buf('p%d' % i) for i in range(4)]
            b_o = [Buf('o0'), Buf('o1')]
            b_dram = Buf('dram')
            g_q = [P.group('q0'), P.group('q1')]
            g_k = [P.group('k0'), P.group('k1')]
            g_v = [P.group('v0'), P.group('v1')]
            g_o = [P.group('o0'), P.group('o1')]
            ci = 0
            step = 0
            for pi_, ps_ in enumerate(passes):
                d, scale = ps_['d'], ps_['scale']
                qt, bq = qT[pi_ % 2], b_q[pi_ % 2]
                P.dma('sp', qt[0:d, :], q_loc[ps_['q0']:ps_['q0'] + d, :], g_q[pi_ % 2], writes=[bq])
                steps = [(c, kb, hf) for c in range(NCORES) for kb in range(16) for hf in range(2)]
                chunk_tiles = {}

                def load_chunk(c):
                    nonlocal ci
                    kt, vt, bk, bv = kc[ci % 2], vc[ci % 2], b_k[ci % 2], b_v[ci % 2]
                    p0 = 0
                    for (r0, n) in ps_['krows']:
                        P.dma('sp', kt[p0:p0 + n, :], kv_all[c * KVROWS + r0:c * KVROWS + r0 + n, :], g_k[ci % 2], writes=[bk])
                        p0 += n
                    vsrc = bass.AP(kv_all.tensor, kv_all.offset + (c * KVROWS + KROWS + 65 * ps_['vs']) * T,
                                   [[1040, 128], [1, 1040]])
                    P.dma('sp', vt[:], vsrc, g_v[ci % 2], writes=[bv])
                    chunk_tiles[c] = (kt, vt, bk, bv)
                    ci += 1

                def qk(i):
                    c, kb, hf = steps[i]
                    if c not in chunk_tiles:
                        load_chunk(c)
                    kt, vt, bk, bv = chunk_tiles[c]
                    sb0 = 4 + 2 * ((step + i) % 2)
                    for j in range(2):
                        q0 = (hf * 2 + j) * 512
                        P.c('pe', 'matmul', reads=[bk, bq], writes=[ps_b[sb0 + j]],
                            out=psum[:, sb0 + j, :], lhsT=kt[0:d, kb * 128:(kb + 1) * 128], rhs=qt[0:d, q0:q0 + 512],
                            start=True, stop=True)

                n = len(steps)
                qk(0)
                for i in range(n):
                    c, kb, hf = steps[i]
                    if i + 1 < n:
                        qk(i + 1)
                    kt, vt, bk, bv = chunk_tiles[c]
                    sb0 = 4 + 2 * ((step + i) % 2)
                    pt, bp = pT[(step + i) % 4], b_p[(step + i) % 4]
                    P.c('act', 'activation', reads=[ps_b[sb0], ps_b[sb0 + 1]], writes=[bp],
                        out=pt[:].rearrange("p (a b) -> p a b", a=2), in_=psum[:, sb0:sb0 + 2, :], func=AF.Exp, scale=scale)
                    first = (c == 0 and kb == 0)
                    last = (c == NCORES - 1 and kb == 15)
                    for j in range(2):
                        ob = hf * 2 + j
                        P.c('pe', 'matmul', reads=[bv, bp], writes=[ps_b[ob]], acc=(not first),
                            out=psum[0:65, ob, :], lhsT=vt[:, kb * 65:(kb + 1) * 65], rhs=pt[:, j * 512:(j + 1) * 512],
                            start=first, stop=last)
                step += n
                ot, bo = osb[pi_ % 2], b_o[pi_ % 2]
                P.c('act', 'activation', reads=ps_b[0:4], writes=[bo],
                    out=ot[:].rearrange("p (a b) -> p a b", a=4), in_=psum[0:65, 0:4, :], func=AF.Copy)
                P.c('dve', 'reciprocal', reads=[bo], writes=[bo], out=ot[64:65, :], in_=ot[64:65, :])
                s = ps_['slot']
                P.dma('pool', mixu[s * 64:(s + 1) * 64, :], ot[0:64, :], g_o[pi_ % 2], reads=[bo], writes=[b_dram])
                P.dma('pool', rden[s:s + 1, :], ot[64:65, :], g_o[pi_ % 2], reads=[bo], writes=[b_dram])
            P.emit()

    def emit_window(self, li, W, q_loc, kv_loc, kv_all, mixu, rden, ctab_d, dtab_d, sel_d, which):
        isC, isD = which == 'C', which == 'D'
        K = self.K
        P = Phase(K, 'win')
        psum, ps_b, par = self.psum, self.ps_b, self.par
        with ExitStack() as es:
            self.cur = es
            KC = self.sb('KC', [64, 2, 18 * 128], BF16) if isC else None
            VC = self.sb('VC', [128, 2, 18, 65], BF16) if isC else None
            KD = self.sb('KD', [64, 4, 20 * 128], BF16) if isD else None
            VD = self.sb('VD', [128, 4, 20, 65], BF16) if isD else None
            QC = self.sb('QC', [64, 4, T], BF16) if isC else None
            QD = self.sb('QD', [64, 4, T], BF16) if isD else None
            cK = [self.sb('cK%d' % i, [64, 4, 256], BF16) for i in range(2)]
            cV = [self.sb('cV%d' % i, [128, 4, 2, 65], BF16) for i in range(2)]
            ctab = self.sb('ctab', [128, 3, 768], F32) if isC else None
            dtab = [self.sb('dtab%d' % i, [128, 3584], F32) for i in range(2)] if isD else None
            ssb = [self.sb('ssb%d' % i, [128, 896], F32) for i in range(2)]
            pT = [self.sb('wpT%d' % i, [128, 896], BF16) for i in range(2)]
            ost = [self.sb('ost%d' % i, [65, 4, 512], F32) for i in range(2)]
            sel = self.sb('sel', [128, 16], F32)
            sk = self.sb('sk', [128, 4], F32)
            b_KC, b_VC, b_KD, b_VD, b_QC, b_QD = [Buf(n) for n in ('KC', 'VC', 'KD', 'VD', 'QC', 'QD')]
            b_cK = [Buf('cK0'), Buf('cK1')]
            b_cV = [Buf('cV0'), Buf('cV1')]
            b_ctab, b_sel, b_sk = Buf('ctab'), Buf('sel'), Buf('sk')
            b_dtab = [Buf('dt0'), Buf('dt1')]
            b_ssb = [Buf('ss0'), Buf('ss1')]
            b_pT = [Buf('wp0'), Buf('wp1')]
            b_ost = [Buf('os0'), Buf('os1')]
            b_dram = Buf('dram')
            g_own, g_tab = P.group('own'), P.group('tab')
            g_cK = [P.group('cK0'), P.group('cK1')]
            g_cV = [P.group('cV0'), P.group('cV1')]
            g_dt = P.group('dt1')
            g_os = [P.group('os0'), P.group('os1')]
            self.load_params(P, W, li)
            P.dma('sp', sel[:], sel_d, g_tab, writes=[b_sel])
            if isC:
                P.dma('sp', ctab[:], ctab_d.rearrange("c p f -> p c f"), g_tab, writes=[b_ctab])
                P.dma('sp', QC[:], q_loc[640:896, :].rearrange("(h p) t -> p h t", p=64), g_own, writes=[b_QC])
                P.dma('sp', KC[:, :, 128:128 + T], kv_loc[544:672, :].rearrange("(h p) t -> p h t", p=64), g_own, writes=[b_KC])
            if isD:
                P.dma('sp', dtab[0][:], dtab_d[li, 0], g_tab, writes=[b_dtab[0]])
                P.dma('sp', QD[:], q_loc[896:1152, :].rearrange("(h p) t -> p h t", p=64), g_own, writes=[b_QD])
                P.dma('sp', KD[:, :, 256:256 + T], kv_loc[672:928, :].rearrange("(h p) t -> p h t", p=64), g_own, writes=[b_KD])

            def vsrc(base, slot0, ns, b0, nbk):
                return bass.AP(base.tensor, base.offset + (KROWS + 65 * slot0) * T + b0 * 65,
                               [[1040, 128], [65 * T, ns], [65, nbk], [1, 65]])
            if isC:
                P.dma('sp', VC[:, :, 1:17, :], vsrc(kv_loc, 8, 2, 0, 16), g_own, writes=[b_VC])
            if isD:
                P.dma('sp', VD[:, :, 2:18, :], vsrc(kv_loc, 10, 4, 0, 16), g_own, writes=[b_VD])
            P.c('act', 'activation', reads=[self.par_b], writes=[b_sk], out=sk[:], in_=par[:, P_SINK:P_SINK + 4], func=AF.Exp)
            cnt = 0
            for side in range(2):
                for c in range(NCORES):
                    scol = side * 8 + c
                    i = cnt % 2
                    cnt += 1
                    base = kv_all[c * KVROWS:(c + 1) * KVROWS, :]
                    t0c = (T - 128) if side == 0 else 0
                    t0d = (T - 256) if side == 0 else 0
                    bC = 15 if side == 0 else 0
                    bD = 14 if side == 0 else 0
                    if isC:
                        P.dma('sp', cK[i][:, 0:2, 0:128], base[544:672, t0c:t0c + 128].rearrange("(h p) t -> p h t", p=64),
                              g_cK[i], writes=[b_cK[i]])
                        dstC = KC[:, :, 0:128] if side == 0 else KC[:, :, 17 * 128:18 * 128]
                        dVC = VC[:, :, 0:1, :] if side == 0 else VC[:, :, 17:18, :]
                    if isD:
                        dstD = KD[:, :, 0:256] if side == 0 else KD[:, :, 18 * 128:20 * 128]
                        dVD = VD[:, :, 0:2, :] if side == 0 else VD[:, :, 18:20, :]

                    def acc(dst, src, bufd, bufs, first, scol=scol):
                        sc = sel[0:dst.shape[0], scol:scol + 1]
                        if first:
                            P.c('dve', 'tensor_scalar', reads=[bufs, b_sel], writes=[bufd],
                                out=dst, in0=src, scalar1=sc, scalar2=None, op0=ALU.mult)
                        else:
                            P.c('dve', 'scalar_tensor_tensor', reads=[bufs, b_sel, bufd], writes=[bufd],
                                out=dst, in0=src, scalar=sc, in1=dst, op0=ALU.mult, op1=ALU.add)
                    if isC:
                        acc(dstC, cK[i][:, 0:2, 0:128], b_KC, b_cK[i], c == 0)
                        P.dma('sp', cV[i][:, 0:2, 0:1, :], vsrc(base, 8, 2, bC, 1), g_cV[i], writes=[b_cV[i]])
                        acc(dVC, cV[i][:, 0:2, 0:1, :], b_VC, b_cV[i], c == 0)
                    if isD:
                        P.dma('sp', cK[i][:, :, :], base[672:928, t0d:t0d + 256].rearrange("(h p) t -> p h t", p=64),
                              g_cK[i], writes=[b_cK[i]])
                        acc(dstD, cK[i][:, :, :], b_KD, b_cK[i], c == 0)
                        P.dma('sp', cV[i][:, :, :, :], vsrc(base, 10, 4, bD, 2), g_cV[i], writes=[b_cV[i]])
                        acc(dVD, cV[i][:, :, :, :], b_VD, b_cV[i], c == 0)

            sc64 = 64 ** -0.5
            wstep = [0]

            def finish(tc, oi, slot0, with_sink):
                ot, bo = ost[oi], b_ost[oi]
                if with_sink:
                    for h in range(4):
                        P.c('dve', 'tensor_scalar', reads=[bo, b_sk], writes=[bo], out=ot[64:65, h, :], in0=ot[64:65, h, :],
                            scalar1=sk[64:65, h:h + 1], scalar2=None, op0=ALU.add)
                P.c('dve', 'reciprocal', reads=[bo], writes=[bo], out=ot[64:65, :, :], in_=ot[64:65, :, :])
                tsl = slice(tc * 512, (tc + 1) * 512)
                P.dma('pool', mixu[slot0 * 64:(slot0 + 4) * 64, tsl].rearrange("(h p) t -> p h t", p=64), ot[0:64, :, :],
                      g_os[oi], reads=[bo], writes=[b_dram])
                P.dma('pool', rden[slot0:slot0 + 4, tsl].rearrange("(o h) t -> o h t", o=1), ot[64:65, :, :],
                      g_os[oi], reads=[bo], writes=[b_dram])

            oi = 0
            for ti in (range(16) if isC else []):
                cls = 1 if ti == 0 else (2 if ti == 15 else 0)
                tq = slice(ti * 128, (ti + 1) * 128)
                ob = 4 + (ti % 2)
                for kv in range(2):
                    w = wstep[0] % 2
                    wstep[0] += 1
                    sb0 = 2 * w
                    fl = flat_ps(psum, sb0, 2)
                    for j in range(3):
                        P.c('pe', 'matmul', reads=[b_KC, b_QC], writes=[ps_b[sb0], ps_b[sb0 + 1]],
                            out=fl[:, j * 256:(j + 1) * 256].rearrange("p (g q) -> p g q", g=2),
                            lhsT=KC[:, kv, (ti + j) * 128:(ti + j + 1) * 128], rhs=QC[:, 2 * kv:2 * kv + 2, tq],
                            start=True, stop=True)
                    P.c('dve', 'scalar_tensor_tensor', reads=[ps_b[sb0], ps_b[sb0 + 1], b_ctab], writes=[b_ssb[w]],
                        out=ssb[w][:, 0:768], in0=fl[:, 0:768], scalar=sc64, in1=ctab[:, cls, :], op0=ALU.mult, op1=ALU.add)
                    P.c('act', 'activation', reads=[b_ssb[w]], writes=[b_pT[w]], out=pT[w][:, 0:768], in_=ssb[w][:, 0:768], func=AF.Exp)
                    for g in range(2):
                        h = 2 * kv + g
                        for j in range(3):
                            P.c('pe', 'matmul', reads=[b_VC, b_pT[w]], writes=[ps_b[ob]], acc=(j > 0),
                                out=psum[0:65, ob, h * 128:(h + 1) * 128], lhsT=VC[:, kv, ti + j, :],
                                rhs=pT[w][:, j * 256 + g * 128:j * 256 + (g + 1) * 128], start=(j == 0), stop=(j == 2))
                tl = ti % 4
                P.c('act', 'activation', reads=[ps_b[ob]], writes=[b_ost[oi]],
                    out=ost[oi][0:65, :, tl * 128:(tl + 1) * 128], in_=psum[0:65, ob, :].rearrange("p (h q) -> p h q", h=4), func=AF.Copy)
                if tl == 3:
                    finish(ti // 4, oi, 12, True)
                    oi = 1 - oi
            for ti in (range(16) if isD else []):
                special = {0: 1, 1: 2, 14: 3, 15: 4}.get(ti)
                if special is not None:
                    P.dma('sp', dtab[1][:], dtab_d[li, special], g_dt, writes=[b_dtab[1]])
                    tb, btb = dtab[1], b_dtab[1]
                else:
                    tb, btb = dtab[0], b_dtab[0]
                j0, j1 = (1, 6) if ti < 2 else ((0, 5) if ti >= 14 else (1, 5))
                nj = j1 - j0 + 1
                tq = slice(ti * 128, (ti + 1) * 128)
                ob = 4 + (ti % 2)
                for h in range(4):
                    w = wstep[0] % 2
                    wstep[0] += 1
                    sb0 = 2 * w
                    fl = flat_ps(psum, sb0, 2)
                    for j in range(j0, j1 + 1):
                        blk = ti - 1 + j
                        P.c('pe', 'matmul', reads=[b_KD, b_QD], writes=[ps_b[sb0], ps_b[sb0 + 1]],
                            out=fl[:, (j - j0) * 128:(j - j0 + 1) * 128], lhsT=KD[:, h, blk * 128:(blk + 1) * 128], rhs=QD[:, h, tq],
                            start=True, stop=True)
                    P.c('dve', 'scalar_tensor_tensor', reads=[ps_b[sb0], ps_b[sb0 + 1], btb], writes=[b_ssb[w]],
                        out=ssb[w][:, 0:nj * 128], in0=fl[:, 0:nj * 128], scalar=sc64,
                        in1=tb[:, (h * 7 + j0) * 128:(h * 7 + j0 + nj) * 128], op0=ALU.mult, op1=ALU.add)
                    P.c('act', 'activation', reads=[b_ssb[w]], writes=[b_pT[w]], out=pT[w][:, 0:nj * 128], in_=ssb[w][:, 0:nj * 128], func=AF.Exp)
                    for j in range(j0, j1 + 1):
                        blk = ti - 1 + j
                        P.c('pe', 'matmul', reads=[b_VD, b_pT[w]], writes=[ps_b[ob]], acc=(j > j0),
                            out=psum[0:65, ob, h * 128:(h + 1) * 128], lhsT=VD[:, h, blk, :],
                            rhs=pT[w][:, (j - j0) * 128:(j - j0 + 1) * 128], start=(j == j0), stop=(j == j1))
                tl = ti % 4
                P.c('act', 'activation', reads=[ps_b[ob]], writes=[b_ost[oi]],
                    out=ost[oi][0:65, :, tl * 128:(tl + 1) * 128], in_=psum[0:65, ob, :].rearrange("p (h q) -> p h q", h=4), func=AF.Copy)
                if tl == 3:
                    finish(ti // 4, oi, 16, False)
                    oi = 1 - oi
            P.emit()

    def emit_mix(self, li, lam_init, W, mixu, rden):
        K = self.K
        P = Phase(K, 'mix')
        psum, ps_b, par, xT = self.psum, self.ps_b, self.par, self.xT
        with ExitStack() as es:
            self.cur = es
            num = self.sb('num', [128, 10, 512], F32)
            rdn = self.sb('rdn', [128, 10, 512], F32)
            yb = self.sb('yb', [128, 2, 512], F32)
            mixn = self.sb('mixn', [128, 8, 512], BF16)
            w_out = self.sb('w_out', [128, 8, D], BF16)
            sq = [self.sb('msq%d' % i, [128, 512], F32) for i in range(2)]
            rstd = self.sb('mrstd', [128, 512], F32)
            lt = self.sb('lt', [128, 40], F32)
            b_num, b_rdn, b_yb, b_mixn, b_w, b_rstd, b_lt = [Buf(n) for n in ('num', 'rdn', 'yb', 'mixn', 'w', 'rstd', 'lt')]
            b_sq = [Buf('sq0'), Buf('sq1')]
            gw, gn, gr = P.group('w'), P.group('num'), P.group('rdn')
            self.load_params(P, W, li)
            for k in range(8):
                P.dma('pool', w_out[:, k, :], W['w_out'][li, k * 128:(k + 1) * 128, :], gw, writes=[b_w])
            for i in range(2):
                P.c('dve', 'tensor_tensor', reads=[self.par_b], writes=[b_lt], out=lt[:, 0:32],
                    in0=par[:, P_LAM + 64 * i:P_LAM + 64 * i + 32], in1=par[:, P_LAM + 64 * i + 32:P_LAM + 64 * i + 64], op=ALU.mult)
                P.c('dve', 'tensor_reduce', reads=[b_lt], writes=[b_lt], out=lt[:, 32 + i:33 + i], in_=lt[:, 0:32],
                    axis=mybir.AxisListType.X, op=ALU.add)
            P.c('act', 'activation', reads=[b_lt], writes=[b_lt], out=lt[:, 34:36], in_=lt[:, 32:34], func=AF.Exp)
            P.c('dve', 'tensor_tensor', reads=[b_lt], writes=[b_lt], out=lt[:, 36:37], in0=lt[:, 35:36], in1=lt[:, 34:35], op=ALU.subtract)
            P.c('dve', 'tensor_tensor', reads=[b_lt, self.par_b], writes=[b_lt], out=lt[:, 37:38], in0=lt[:, 36:37],
                in1=par[:, P_LI:P_LI + 1], op=ALU.subtract)
            P.c('dve', 'tensor_tensor', reads=[b_lt, self.par_b], writes=[b_lt], out=lt[:, 38:39], in0=par[:, P_GSUB:P_GSUB + 1],
                in1=par[:, P_OML:P_OML + 1], op=ALU.mult)
            bank = [0]

            def nb():
                bank[0] = bank[0] % 7 + 1
                return bank[0]
            for tc in range(NTC):
                tsl = slice(tc * 512, (tc + 1) * 512)
                P.dma('sp', num[:], mixu[:, tsl].rearrange("(q p) t -> p q t", p=128), gn, writes=[b_num])
                for s in range(NSLOT):
                    src = bass.AP(rden.tensor, rden.offset + s * T + tc * 512, [[0, 64], [1, 512]])
                    P.dma('sp', rdn[(s % 2) * 64:(s % 2) * 64 + 64, s // 2, :], src, gr, writes=[b_rdn])
                P.c('dve', 'tensor_tensor', reads=[b_num, b_rdn], writes=[b_num], out=num[:], in0=num[:], in1=rdn[:], op=ALU.mult)
                for i in range(2):
                    P.c('dve', 'scalar_tensor_tensor', reads=[b_num, b_lt], writes=[b_yb], out=yb[:, i, :], in0=num[:, 4 + i, :],
                        scalar=lt[:, 37:38], in1=num[:, 2 + i, :], op0=ALU.mult, op1=ALU.add)

                def gnorm(chunks, gcol, mo):
                    srcs = [(ap, 128, [bb]) for ap, bb in chunks]
                    self.rstd_from(P, srcs, 256.0, 0, rstd[:], b_rstd, sq, b_sq)
                    for i, (ap, bb) in enumerate(chunks):
                        P.c('dve', 'scalar_tensor_tensor', reads=[bb, self.par_b, b_rstd], writes=[b_mixn], out=mixn[:, mo + i, :],
                            in0=ap, scalar=par[:, gcol + i:gcol + i + 1], in1=rstd[:], op0=ALU.mult, op1=ALU.mult)
                gnorm([(num[:, 0, :], b_num), (num[:, 1, :], b_num)], P_GA, 0)
                for i in range(2):
                    self.rstd_from(P, [(yb[:, i, :], 128, [b_yb])], 64.0, 0, rstd[:], b_rstd, sq, b_sq, lhs=self.bd_f)
                    P.c('dve', 'scalar_tensor_tensor', reads=[b_yb, b_lt, b_rstd], writes=[b_mixn], out=mixn[:, 2 + i, :],
                        in0=yb[:, i, :], scalar=lt[:, 38:39], in1=rstd[:], op0=ALU.mult, op1=ALU.mult)
                gnorm([(num[:, 6, :], b_num), (num[:, 7, :], b_num)], P_GC, 4)
                gnorm([(num[:, 8, :], b_num), (num[:, 9, :], b_num)], P_GD, 6)
                for oc in range(8):
                    pi = nb()
                    for k in range(8):
                        P.c('pe', 'matmul', reads=[b_w, b_mixn], writes=[ps_b[pi]], acc=(k > 0), out=psum[:, pi, :],
                            lhsT=w_out[:, k, oc * 128:(oc + 1) * 128], rhs=mixn[:, k, :], start=(k == 0), stop=(k == 7))
                    P.c('dve', 'tensor_tensor', reads=[ps_b[pi], self.xT_b[oc][tc]], writes=[self.xT_b[oc][tc]],
                        out=xT[:, oc, tsl], in0=xT[:, oc, tsl], in1=psum[:, pi, :], op=ALU.add)
            P.emit()

    def emit_ffn(self, li, W):
        K = self.K
        P = Phase(K, 'ffn')
        psum, ps_b, par, xT = self.psum, self.ps_b, self.par, self.xT
        with ExitStack() as es:
            self.cur = es
            h2 = self.sb('h2', [128, 8, T], BF16)
            wu = [self.sb('wu%d' % i, [128, 8, 1024], BF16) for i in range(2)]
            wd = [self.sb('wd%d' % i, [128, 8, D], BF16) for i in range(2)]
            aT = [self.sb('aT%d' % i, [128, 8, 512], BF16) for i in range(2)]
            rl = [self.sb('rl%d' % i, [128, 512], F32) for i in range(2)]
            sq = [self.sb('fsq%d' % i, [128, 512], F32) for i in range(2)]
            rstd = self.sb('frstd', [128, 512], F32)
            b_h2 = [Buf('h2_%d' % i) for i in range(NTC)]
            b_wu, b_wd = [Buf('wu0'), Buf('wu1')], [Buf('wd0'), Buf('wd1')]
            b_aT = [[Buf('aT%d_%d' % (i, f)) for f in range(8)] for i in range(2)]
            b_rl = [Buf('rl0'), Buf('rl1')]
            b_sq = [Buf('sq0'), Buf('sq1')]
            b_rstd = Buf('rstd')
            g_wu, g_wd = [P.group('wu0'), P.group('wu1')], [P.group('wd0'), P.group('wd1')]
            self.load_params(P, W, li)
            for tc in range(NTC):
                tsl = slice(tc * 512, (tc + 1) * 512)
                srcs = [(xT[:, c, tsl], 128, [self.xT_b[c][tc]]) for c in range(8)]
                self.rstd_from(P, srcs, float(D), 0, rstd[:], b_rstd, sq, b_sq)
                for c in range(8):
                    P.c('dve', 'scalar_tensor_tensor', reads=[self.xT_b[c][tc], self.par_b, b_rstd], writes=[b_h2[tc]],
                        out=h2[:, c, tsl], in0=xT[:, c, tsl], scalar=par[:, P_G2 + c:P_G2 + c + 1], in1=rstd[:],
                        op0=ALU.mult, op1=ALU.mult)
            ub, db, st = 0, 0, 0
            for fq in range(4):
                w = fq % 2
                for k in range(8):
                    P.dma('pool', wu[w][:, k, :], W['w_up'][li, k * 128:(k + 1) * 128, fq * 1024:(fq + 1) * 1024], g_wu[w], writes=[b_wu[w]])
                for k in range(8):
                    P.dma('pool', wd[w][:, k, :], W['w_down'][li, fq * 1024 + k * 128:fq * 1024 + (k + 1) * 128, :], g_wd[w], writes=[b_wd[w]])
                for tc in range(NTC):
                    tsl = slice(tc * 512, (tc + 1) * 512)
                    a = st % 2
                    st += 1
                    for fc in range(8):
                        pi = 1 + ub % 3
                        ub += 1
                        for k in range(8):
                            P.c('pe', 'matmul', reads=[b_wu[w], b_h2[tc]], writes=[ps_b[pi]], acc=(k > 0), out=psum[:, pi, :],
                                lhsT=wu[w][:, k, fc * 128:(fc + 1) * 128], rhs=h2[:, k, tsl], start=(k == 0), stop=(k == 7))
                        r = (ub) % 2
                        P.c('act', 'activation', reads=[ps_b[pi]], writes=[b_rl[r]], out=rl[r][:], in_=psum[:, pi, :], func=AF.Relu)
                        P.c('pool', 'tensor_tensor', reads=[b_rl[r]], writes=[b_aT[a][fc]], out=aT[a][:, fc, :], in0=rl[r][:], in1=rl[r][:], op=ALU.mult)
                    for oc in range(8):
                        pi = 4 + db % 4
                        db += 1
                        for k in range(8):
                            P.c('pe', 'matmul', reads=[b_wd[w], b_aT[a][k]], writes=[ps_b[pi]], acc=(k > 0), out=psum[:, pi, :],
                                lhsT=wd[w][:, k, oc * 128:(oc + 1) * 128], rhs=aT[a][:, k, :], start=(k == 0), stop=(k == 7))
                        P.c('dve', 'tensor_tensor', reads=[ps_b[pi], self.xT_b[oc][tc]], writes=[self.xT_b[oc][tc]],
                            out=xT[:, oc, tsl], in0=xT[:, oc, tsl], in1=psum[:, pi, :], op=ALU.add)
            P.emit()

    def emit_final(self, gfin_d, out_d, xout_d=None):
        K = self.K
        P = Phase(K, 'fin')
        xT = self.xT
        with ExitStack() as es:
            self.cur = es
            gf = self.sb('gf', [128, 8], F32)
            yo = [self.sb('yo%d' % i, [128, 8, 512], F32) for i in range(2)]
            sq = [self.sb('nsq%d' % i, [128, 512], F32) for i in range(2)]
            rstd = self.sb('nrstd', [128, 512], F32)
            b_gf, b_rstd = Buf('gf'), Buf('rstd')
            b_yo = [Buf('yo0'), Buf('yo1')]
            b_sq = [Buf('sq0'), Buf('sq1')]
            b_dram = Buf('dram')
            gg = P.group('gf')
            gy = [P.group('yo0'), P.group('yo1')]
            gx = P.group('xo')
            P.dma('sp', gf[:], gfin_d, gg, writes=[b_gf])
            if xout_d is not None:
                for c in range(8):
                    P.dma('pool', xout_d[c * 128:(c + 1) * 128, :], xT[:, c, :], gx, reads=list(self.xT_b[c]), writes=[b_dram])
            for tc in range(NTC):
                tsl = slice(tc * 512, (tc + 1) * 512)
                srcs = [(xT[:, c, tsl], 128, [self.xT_b[c][tc]]) for c in range(8)]
                self.rstd_from(P, srcs, float(D), 0, rstd[:], b_rstd, sq, b_sq)
                y = yo[tc % 2]
                for c in range(8):
                    P.c('dve', 'scalar_tensor_tensor', reads=[self.xT_b[c][tc], b_gf, b_rstd], writes=[b_yo[tc % 2]],
                        out=y[:, c, :], in0=xT[:, c, tsl], scalar=gf[:, c:c + 1], in1=rstd[:], op0=ALU.mult, op1=ALU.mult)
                P.dma('sp', out_d[:, tsl].rearrange("(c p) t -> p c t", p=128), y[:], gy[tc % 2], reads=[b_yo[tc % 2]], writes=[b_dram])
            P.emit()


W1_SHAPES = dict(w_fm=[D, NFM], w_tm=[D, NTM], w_uq=[192, 384], w_uqs=[192, 384], w_uk=[128, 256], w_uv=[128, 256],
                 params=[128, NPAR_L])
W2_SHAPES = dict(w_out=[D, D], w_up=[D, DFF], w_down=[DFF, D], params=[128, NPAR_L])


def build_l1():
    nc = bass.Bass("TRN2", target_bir_lowering=False)
    st = ExitStack()
    B = Builder(nc, st)
    W = {k: nc.dram_tensor(k, [1] + s, F32, kind="ExternalInput").ap() for k, s in W1_SHAPES.items()}
    xT_d = nc.dram_tensor("xT_in", [D, T], F32, kind="ExternalInput").ap()
    rope = nc.dram_tensor("rope", [4, 128, T], F32, kind="ExternalInput").ap()
    q_loc = nc.dram_tensor("q_loc", [QROWS, T], BF16, kind="ExternalOutput").ap()
    kv_loc = nc.dram_tensor("kv_loc", [KVROWS, T], BF16, kind="ExternalOutput").ap()
    B.setup_persistent()
    B.emit_init(xT_d)
    B.emit_phase1(0, W, q_loc, kv_loc, rope)
    return nc, st


def build_l2(lam_init, final):
    nc = bass.Bass("TRN2", target_bir_lowering=False)
    st = ExitStack()
    B = Builder(nc, st)
    W = {k: nc.dram_tensor(k, [1] + s, F32, kind="ExternalInput").ap() for k, s in W2_SHAPES.items()}
    xT_d = nc.dram_tensor("xT_in", [D, T], F32, kind="ExternalInput").ap()
    q_loc = nc.dram_tensor("q_loc", [QROWS, T], BF16, kind="ExternalInput").ap()
    kv_loc = nc.dram_tensor("kv_loc", [KVROWS, T], BF16, kind="ExternalInput").ap()
    kv_all = nc.dram_tensor("kv_all", [NCORES * KVROWS, T], BF16, kind="ExternalInput").ap()
    ctab = nc.dram_tensor("ctab", [3, 128, 768], F32, kind="ExternalInput").ap()
    dtab = nc.dram_tensor("dtab", [1, 5, 128, 3584], F32, kind="ExternalInput").ap()
    sel = nc.dram_tensor("sel", [128, 16], F32, kind="ExternalInput").ap()
    gfin = nc.dram_tensor("gfin", [128, 8], F32, kind="ExternalInput").ap()
    mixu = nc.dram_tensor("mixu", [NSLOT * 64, T], F32, kind="Internal").ap()
    rden = nc.dram_tensor("rden", [NSLOT, T], F32, kind="Internal").ap()
    xout = nc.dram_tensor("xT_out", [D, T], F32, kind="ExternalOutput").ap()
    yout = nc.dram_tensor("yT_out", [D, T], F32, kind="ExternalOutput").ap()
    B.setup_persistent()
    B.emit_init(xT_d)
    B.emit_dense(q_loc, kv_all, mixu, rden)
    B.emit_window(0, W, q_loc, kv_loc, kv_all, mixu, rden, ctab, dtab, sel, 'C')
    B.emit_window(0, W, q_loc, kv_loc, kv_all, mixu, rden, ctab, dtab, sel, 'D')
    B.emit_mix(0, lam_init, W, mixu, rden)
    B.emit_ffn(0, W)
    B.emit_final(gfin, yout, xout)
    return nc, st


def kernel(**inp):
    inp = {k: np.asarray(v) for k, v in inp.items()}
    x = inp['x'][0]
    cores = list(range(NCORES))
    xT = [np.ascontiguousarray(x[r * T:(r + 1) * T].T) for r in cores]
    ropes = [_rope_tables(r) for r in cores]
    ctabs = [_c_tables(r) for r in cores]
    sels = [_prep_core_sel(r) for r in cores]
    y = None
    nc1, st1 = build_l1()
    nc2, st2 = build_l2(0.0, True)
    for l in range(L):
        sh = _prep_shared(inp, [l])
        in1 = []
        for r in cores:
            m = {k: sh[k] for k in W1_SHAPES}
            m['xT_in'] = xT[r]
            m['rope'] = ropes[r]
            in1.append(m)
        r1 = run_bass_kernel_spmd(nc1, in1, core_ids=cores).results
        kv_all = np.ascontiguousarray(np.concatenate([r1[r]['kv_loc'] for r in cores], axis=0))
        in2 = []
        for r in cores:
            m = {k: sh[k] for k in W2_SHAPES}
            m['xT_in'] = xT[r]
            m['q_loc'] = r1[r]['q_loc']
            m['kv_loc'] = r1[r]['kv_loc']
            m['kv_all'] = kv_all
            m['ctab'] = ctabs[r]
            m['dtab'] = _d_tables(inp['na_rpb'], r, [l])
            m['sel'] = sels[r]
            m['gfin'] = sh['gfin']
            in2.append(m)
        r2 = run_bass_kernel_spmd(nc2, in2, core_ids=cores).results
        xT = [np.asarray(r2[r]['xT_out']) for r in cores]
        y = [np.asarray(r2[r]['yT_out']) for r in cores]
    st1.close()
    st2.close()
    out = np.concatenate([yy.T for yy in y], axis=0)[None].astype(np.float32)
    return out


def _prep_core_sel(rank):
    sel = np.zeros((128, 16), np.float32)
    if rank > 0:
        sel[:, rank - 1] = 1.0
    if rank < NCORES - 1:
        sel[:, 8 + rank + 1] = 1.0
    return sel
```

```python
import math
from contextlib import ExitStack
import numpy as np
import concourse.bass as bass
import concourse.mybir as mybir
from concourse.bass_utils import run_bass_kernel_spmd

F32 = mybir.dt.float32
BF16 = mybir.dt.bfloat16
AF = mybir.ActivationFunctionType
ALU = mybir.AluOpType

NCORES = 8
L = 4
LAMBDA_INIT_C = [0.8 - 0.6 * math.exp(-0.3 * l) for l in range(4)]
D = 1024
S = 16384
T = 2048
NTC = 4
DFF = 4096
EPS = 1e-6
NEG = -30000.0
KROWS = 928
VROWS = 910
KVROWS = KROWS + VROWS
QROWS = 1152
NSLOT = 20
NFM = 2688
NTM = 640


def _sw(n):
    h = n // 2
    return np.concatenate([np.arange(h, n), np.arange(0, h)])


def _fm_cols():
    o_cq, o_ckv, o_kr, o_bq, o_bk, o_bv, o_cq2, o_ck, o_cv, o_dq, o_dk, o_dv = (
        0, 192, 320, 352, 608, 864, 1120, 1376, 1504, 1632, 1888, 2144)
    sw32, sw64 = _sw(32), _sw(64)
    cols = []
    cols += list(range(o_cq, o_cq + 192))
    cols += list(range(o_ckv, o_ckv + 128))
    cols += list(range(o_kr, o_kr + 32))
    cols += list(o_kr + sw32)
    bq = [o_bq + m * 32 + i for m in range(8) for i in range(32)]
    bqs = [o_bq + m * 32 + i for m in range(8) for i in sw32]
    bk = [o_bk + m * 32 + i for m in range(8) for i in range(32)]
    bks = [o_bk + m * 32 + i for m in range(8) for i in sw32]
    cq = [o_cq2 + h * 64 + i for h in range(4) for i in range(64)]
    cqs = [o_cq2 + h * 64 + i for h in range(4) for i in sw64]
    ck = [o_ck + h * 64 + i for h in range(2) for i in range(64)]
    cks = [o_ck + h * 64 + i for h in range(2) for i in sw64]
    cols += bq + bqs + bk + bks + cq + cqs + ck + cks
    cols += list(range(o_dq, o_dq + 256)) + list(range(o_dk, o_dk + 256))
    tm = list(range(o_bv, o_bv + 256)) + list(range(o_cv, o_cv + 128)) + list(range(o_dv, o_dv + 256))
    return np.array(cols), np.array(tm)


def _rope_tables(rank):
    pos = (rank * T + np.arange(T)).astype(np.float32)
    out = np.zeros((4, 128, T), np.float32)
    for ti, dim in ((0, 32), (2, 64)):
        inv = (1.0 / (np.float32(10000.0) ** (np.arange(0, dim, 2, dtype=np.float32) / np.float32(dim)))).astype(np.float32)
        ang = (pos[:, None] * inv[None, :]).astype(np.float32)
        c, s = np.cos(ang).astype(np.float32), np.sin(ang).astype(np.float32)
        half = dim // 2
        for p in range(128):
            i = p % dim
            out[ti, p] = c[:, i % half]
            out[ti + 1, p] = (-s[:, i % half]) if i < half else s[:, i % half]
    return out


def _c_tables(rank):
    ki = np.arange(128)[:, None]
    qi = np.arange(128)[None, :]
    base = np.zeros((128, 3, 2, 128), np.float32)
    base[:, 0] = np.where(qi <= ki, 0.0, NEG)[:, None, :]
    base[:, 2] = np.where(ki <= qi, 0.0, NEG)[:, None, :]
    t0 = base.copy()
    t15 = base.copy()
    if rank == 0:
        t0[:, 0] = NEG
    if rank == NCORES - 1:
        t15[:, 2] = NEG
    return np.stack([base, t0, t15]).reshape(3, 128, 768)


def _d_table(rpb_l, r0):
    ki = np.arange(128)
    kr, kc = ki // 64, ki % 64
    qi = np.arange(128)
    qr, qc = qi // 64, qi % 64
    out = np.full((128, 4, 7, 128), NEG, np.float32)
    rq = r0 + qr
    rs = np.clip(rq - 4, 0, 256 - 8)
    cs = np.clip(qc - 8, 0, 64 - 16)
    for j in range(7):
        krow = r0 - 6 + 2 * j + kr
        vr = (krow[:, None] >= rs[None, :]) & (krow[:, None] < rs[None, :] + 8) & (krow[:, None] >= 0) & (krow[:, None] < 256)
        vc = (kc[:, None] >= cs[None, :]) & (kc[:, None] < cs[None, :] + 16)
        valid = vr & vc
        dr = np.clip(krow[:, None] - rq[None, :] + 7, 0, 14)
        dc = np.clip(kc[:, None] - qc[None, :] + 15, 0, 30)
        for h in range(4):
            out[:, h, j, :] = np.where(valid, rpb_l[h][dr, dc], NEG)
    return out.reshape(128, 4 * 7 * 128)


def _d_tables(rpb, rank, layers):
    res = []
    for l in layers:
        cls = []
        for ti in (5, 0, 1, 14, 15):
            r0 = rank * 32 + 2 * ti
            cls.append(_d_table(rpb[l], r0))
        res.append(np.stack(cls))
    return np.stack(res)


P_G1, P_G2, P_GQ, P_GKV, P_GA, P_GC, P_GD, P_GSUB, P_LAM, P_SINK = 0, 8, 16, 18, 19, 21, 23, 25, 26, 154
NPAR_L = 160
P_LI, P_OML = 158, 159
P_GF = 0


def _params(inp, layers):
    out = np.zeros((len(layers), 128, NPAR_L), np.float32)
    for i, l in enumerate(layers):
        out[i, :, P_G1:P_G1 + 8] = inp['norm1_g'][l].reshape(8, 128).T
        out[i, :, P_G2:P_G2 + 8] = inp['norm2_g'][l].reshape(8, 128).T
        gq = np.zeros(256, np.float32)
        gq[:192] = inp['mla_q_norm_g'][l]
        out[i, :, P_GQ:P_GQ + 2] = gq.reshape(2, 128).T
        out[i, :, P_GKV] = inp['mla_kv_norm_g'][l]
        out[i, :, P_GA:P_GA + 2] = inp['out_g_mla'][l].reshape(2, 128).T
        out[i, :, P_GC:P_GC + 2] = inp['out_g_swa'][l].reshape(2, 128).T
        out[i, :, P_GD:P_GD + 2] = inp['out_g_na'][l].reshape(2, 128).T
        out[i, :, P_GSUB] = np.tile(inp['diff_subln_g'][l], 2)
        lam = np.stack([inp['diff_lambda_q1'][l], inp['diff_lambda_k1'][l],
                        inp['diff_lambda_q2'][l], inp['diff_lambda_k2'][l]]).reshape(-1)
        out[i, :, P_LAM:P_LAM + 128] = lam[None, :]
        out[i, :, P_SINK:P_SINK + 4] = inp['swa_sinks'][l][None, :]
        out[i, :, P_LI] = LAMBDA_INIT_C[l]
        out[i, :, P_OML] = 1.0 - LAMBDA_INIT_C[l]
    return out


def _prep_shared(inp, layers):
    fm, tm = _fm_cols()
    w_in = inp['w_in']
    d = {}
    d['w_fm'] = np.ascontiguousarray(np.stack([w_in[l][:, fm] for l in layers]))
    d['w_tm'] = np.ascontiguousarray(np.stack([w_in[l][:, tm] for l in layers]))
    uqs_cols = np.concatenate([np.concatenate([h * 96 + np.arange(64), h * 96 + 64 + _sw(32)]) for h in range(4)])
    d['w_uq'] = np.ascontiguousarray(np.stack([inp['mla_w_uq'][l] for l in layers]))
    d['w_uqs'] = np.ascontiguousarray(np.stack([inp['mla_w_uq'][l][:, uqs_cols] for l in layers]))
    d['w_uk'] = np.ascontiguousarray(np.stack([inp['mla_w_uk'][l] for l in layers]))
    d['w_uv'] = np.ascontiguousarray(np.stack([inp['mla_w_uv'][l] for l in layers]))
    d['w_out'] = np.ascontiguousarray(np.stack([inp['w_out'][l] for l in layers]))
    d['w_up'] = np.ascontiguousarray(np.stack([inp['w_up'][l] for l in layers]))
    d['w_down'] = np.ascontiguousarray(np.stack([inp['w_down'][l] for l in layers]))
    d['params'] = _params(inp, layers)
    d['gfin'] = np.ascontiguousarray(inp['final_norm_g'].reshape(8, 128).T)
    return d


def _prep_core(inp, rank, layers):
    d = {}
    d['rope'] = _rope_tables(rank)
    d['ctab'] = _c_tables(rank)
    d['dtab'] = _d_tables(inp['na_rpb'], rank, layers)
    sel = np.zeros((128, 16), np.float32)
    if rank > 0:
        sel[:, rank - 1] = 1.0
    if rank < NCORES - 1:
        sel[:, 8 + rank + 1] = 1.0
    d['sel'] = sel
    return d


ENGS = ('pe', 'act', 'dve', 'pool', 'sp')


class Buf:
    def __init__(self, name):
        self.name = name
        self.wev = {}
        self.readers = []
        self.war = []


class SemGroup:
    def __init__(self, K, name):
        self.K = K
        self.name = name
        self.sem = None
        self.cnt = 0

    def get(self):
        if self.sem is None:
            self.sem = self.K.alloc_sem()
            self.cnt = self.K.sem_base[id(self.sem)]
        return self.sem


class Op:
    __slots__ = ('eng', 'fn', 'deps', 'signal', 'ticket', 'dma', 'sg', 'cnt', 'phase')

    def __init__(self, eng, fn):
        self.eng, self.fn = eng, fn
        self.deps = []
        self.signal = False
        self.ticket = None
        self.dma = False
        self.sg = None
        self.cnt = 0


class Kern:
    def __init__(self, nc, stack):
        self.nc = nc
        self.stack = stack
        self.csem = {e: stack.enter_context(nc.semaphore('c_' + e)) for e in ('pe', 'act', 'dve', 'pool')}
        self.ccnt = {e: 0 for e in self.csem}
        self.free_sems = []
        self.sem_base = {}
        self.nsem = 0
        self.all_groups = []

    def alloc_sem(self):
        if self.free_sems:
            return self.free_sems.pop()
        s = self.stack.enter_context(self.nc.semaphore('d%d' % self.nsem))
        self.nsem += 1
        self.sem_base[id(s)] = 0
        return s


class Phase:
    def __init__(self, K, name):
        self.K = K
        self.nc = K.nc
        self.name = name
        self.ops = []
        self.groups = []

    def group(self, name):
        g = SemGroup(self.K, name)
        g.phase = self
        self.groups.append(g)
        return g

    def _wevents(self, b):
        ev = []
        if 'c' in b.wev:
            ev.append(('c', b.wev['c']))
        if 'd' in b.wev:
            ev.append(('d',) + b.wev['d'])
        return ev

    def op(self, eng, fn, reads=(), writes=(), acc=False):
        o = Op(eng, fn)
        for b in reads:
            o.deps += self._wevents(b)
        for b in writes:
            for ev in self._wevents(b):
                if acc and ev[0] == 'c' and ev[1].eng == 'pe' and eng == 'pe':
                    continue
                o.deps.append(ev)
            o.deps += b.readers + b.war
        for b in reads:
            b.readers.append(('c', o))
        for b in writes:
            if b.readers:
                b.war = b.readers
                b.readers = []
            b.wev = {'c': o}
        self.ops.append(o)
        return o

    def dma(self, q, out, in_, sg, reads=(), writes=()):
        def fn(e, out=out, in_=in_):
            return e.dma_start(out=out, in_=in_)
        o = Op(q, fn)
        o.dma = True
        for b in reads:
            o.deps += self._wevents(b)
        for b in writes:
            if 'c' in b.wev:
                o.deps.append(('c', b.wev['c']))
            o.deps += b.readers + b.war
        sg.get()
        sg.cnt += 16
        o.sg, o.cnt = sg, sg.cnt
        ev = ('d', sg, sg.cnt)
        for b in reads:
            b.readers.append(ev)
        for b in writes:
            if b.readers:
                b.war = b.readers
                b.readers = []
            b.wev['d'] = (sg, sg.cnt)
        self.ops.append(o)
        return o

    def c(self, eng, method, reads=(), writes=(), acc=False, **kw):
        def fn(e, method=method, kw=kw):
            return getattr(e, method)(**kw)
        return self.op(eng, fn, reads, writes, acc)

    def emit(self):
        K = self.K
        nc = self.nc
        for o in self.ops:
            o.phase = self
        for o in self.ops:
            o.deps = [ev for ev in o.deps if ev[1].phase is self]
            for ev in o.deps:
                if ev[0] == 'c':
                    ev[1].signal = True
        for o in self.ops:
            if not o.dma and o.signal:
                K.ccnt[o.eng] += 1
                o.ticket = K.ccnt[o.eng]
        per = {e: [] for e in ENGS}
        for o in self.ops:
            per[o.eng].append(o)
        final_waits = [(g.sem, g.cnt) for g in self.groups if g.sem is not None]

        def run(ename, e):
            seen = {}
            for o in per[ename]:
                need = {}
                for ev in o.deps:
                    if ev[0] == 'c':
                        p = ev[1]
                        if p.eng == 'pe' and ename == 'pe':
                            continue
                        sem, val = K.csem[p.eng], p.ticket
                    else:
                        sem, val = ev[1].sem, ev[2]
                    k = id(sem)
                    if seen.get(k, -1) >= val:
                        continue
                    if k not in need or need[k][1] < val:
                        need[k] = (sem, val)
                for k, (sem, val) in need.items():
                    e.wait_ge(sem, val)
                    seen[k] = val
                inst = o.fn(e)
                if o.dma:
                    inst.then_inc(o.sg.sem, 16)
                elif o.signal:
                    inst.then_inc(K.csem[o.eng], 1)
            if ename == 'sp':
                for sem, val in final_waits:
                    e.wait_ge(sem, val)
                for en in ('pe', 'act', 'dve', 'pool'):
                    if K.ccnt[en] > 0:
                        e.wait_ge(K.csem[en], K.ccnt[en])

        with nc.Block() as block:
            @block.sync
            def _(e):
                run('sp', e)

            @block.tensor
            def _(e):
                run('pe', e)

            @block.scalar
            def _(e):
                run('act', e)

            @block.vector
            def _(e):
                run('dve', e)

            @block.gpsimd
            def _(e):
                run('pool', e)
        for g in self.groups:
            if g.sem is not None:
                K.sem_base[id(g.sem)] = g.cnt
                K.free_sems.append(g.sem)
                g.sem = None
        self.ops = []


FM_GROUPS = [
    ('cq0', 0, 128), ('cq1', 128, 64), ('ckv', 192, 128), ('kr', 320, 32), ('krs', 352, 32),
    ('bq0', 384, 128), ('bq1', 512, 128), ('bqs0', 640, 128), ('bqs1', 768, 128),
    ('bk0', 896, 128), ('bk1', 1024, 128), ('bks0', 1152, 128), ('bks1', 1280, 128),
    ('cq0_', 1408, 128), ('cq1_', 1536, 128), ('cqs0', 1664, 128), ('cqs1', 1792, 128),
    ('ck', 1920, 128), ('cks', 2048, 128),
    ('dq0', 2176, 128), ('dq1', 2304, 128), ('dk0', 2432, 128), ('dk1', 2560, 128),
]
FM_OFF = {n: (o, m) for n, o, m in FM_GROUPS}
ST_KB, ST_KC, ST_KD, ST_QB, ST_QC, ST_QD = 0, 2, 3, 5, 7, 9
LAMBDA_INIT = [0.8 - 0.6 * math.exp(-0.3 * l) for l in range(L)]


def flat_ps(psum, b0, nb):
    return psum[:, b0:b0 + nb, :].rearrange("p a b -> p (a b)")


class Builder:
    def __init__(self, nc, stack):
        self.nc = nc
        self.stack = stack
        self.K = Kern(nc, stack)

    def sb(self, name, shape, dtype):
        self.uid = getattr(self, 'uid', 0) + 1
        return self.cur.enter_context(self.nc.sbuf_tensor('s%d_%s' % (self.uid, name), list(shape), dtype))

    def setup_persistent(self):
        nc, st = self.nc, self.stack
        self.xT = st.enter_context(nc.sbuf_tensor('xT', [128, 8, T], F32))
        self.xT_b = [[Buf('xT_%d_%d' % (c, t)) for t in range(NTC)] for c in range(8)]
        self.par = st.enter_context(nc.sbuf_tensor('par', [128, NPAR_L], F32))
        self.par_b = Buf('par')
        self.cst = st.enter_context(nc.sbuf_tensor('cst', [128, 4], F32))
        self.ones_f = st.enter_context(nc.sbuf_tensor('ones_f', [128, 128], F32))
        self.bd_f = st.enter_context(nc.sbuf_tensor('bd_f', [128, 128], F32))
        self.cst_b = Buf('cst')
        self.psum = st.enter_context(nc.psum_tensor('psum', [128, 8, 512], F32))
        self.ps_b = [Buf('ps%d' % i) for i in range(8)]

    def emit_init(self, xT_dram):
        P = Phase(self.K, 'init')
        g = P.group('xload')
        for c in range(8):
            P.dma('sp', self.xT[:, c, :], xT_dram[c * 128:(c + 1) * 128, :], g, writes=list(self.xT_b[c]))
        P.c('dve', 'memset', writes=[self.cst_b], ap=self.cst[:, 0:1], constant=EPS)
        P.c('dve', 'memset', writes=[self.cst_b], ap=self.cst[:, 1:2], constant=1.0)
        P.c('dve', 'memset', writes=[self.cst_b], ap=self.ones_f[:], constant=1.0)
        P.c('dve', 'memset', writes=[self.cst_b], ap=self.bd_f[:], constant=0.0)
        P.c('dve', 'memset', writes=[self.cst_b], ap=self.bd_f[0:64, 0:64], constant=1.0)
        P.c('dve', 'memset', writes=[self.cst_b], ap=self.bd_f[64:128, 64:128], constant=1.0)
        P.emit()

    def load_params(self, P, W, li):
        gp = P.group('par')
        P.dma('sp', self.par[:], W['params'][li], gp, writes=[self.par_b])

    def rstd_from(self, P, srcs, n, ps_i, rstd_ap, rstd_b, sq_tiles, sq_bufs, lhs=None):
        ps = self.psum[:, ps_i, :]
        lhs_t = self.ones_f if lhs is None else lhs
        ns = len(srcs)
        for i, (ap, kk, bufs) in enumerate(srcs):
            sq, sqb = sq_tiles[i % 2], sq_bufs[i % 2]
            P.c('act', 'activation', reads=bufs, writes=[sqb], out=sq[0:kk, :], in_=ap, func=AF.Square)
            P.c('pe', 'matmul', reads=[sqb, self.cst_b], writes=[self.ps_b[ps_i]], acc=(i > 0),
                out=ps, lhsT=lhs_t[0:kk, :], rhs=sq[0:kk, :], start=(i == 0), stop=(i == ns - 1))
        tmp = sq_tiles[0]
        P.c('act', 'activation', reads=[self.ps_b[ps_i], self.cst_b], writes=[sq_bufs[0]],
            out=tmp[:], in_=ps, func=AF.Sqrt, bias=self.cst[:, 0:1], scale=1.0 / n)
        P.c('dve', 'reciprocal', reads=[sq_bufs[0]], writes=[rstd_b], out=rstd_ap, in_=tmp[:])

    def emit_phase1(self, li, W, q_loc, kv_loc, rope):
        K = self.K
        P = Phase(K, 'p1')
        psum, ps_b, xT, par = self.psum, self.ps_b, self.xT, self.par
        with ExitStack() as es:
            self.cur = es
            w_fm = self.sb('w_fm', [128, 8, NFM], BF16)
            w_tm = self.sb('w_tm', [128, 8, NTM], BF16)
            w_uq = self.sb('w_uq', [128, 2, 384], BF16)
            w_uqs = self.sb('w_uqs', [128, 2, 384], BF16)
            w_uk = self.sb('w_uk', [128, 256], BF16)
            w_uv = self.sb('w_uv', [128, 256], BF16)
            tab = self.sb('tab', [128, 4, 512], F32)
            hT = self.sb('hT', [128, 8, 512], BF16)
            sq = [self.sb('sq%d' % i, [128, 512], F32) for i in range(2)]
            rstd = self.sb('rstd', [128, 512], F32)
            cqf = self.sb('cqf', [128, 2, 512], F32)
            cqn = self.sb('cqn', [128, 2, 512], BF16)
            ckf = self.sb('ckf', [128, 512], F32)
            cn = self.sb('cn', [128, 512], BF16)
            t1 = [self.sb('t1_%d' % i, [128, 512], F32) for i in range(2)]
            t2 = [self.sb('t2_%d' % i, [128, 512], F32) for i in range(2)]
            st128 = self.sb('st128', [128, 11, 512], BF16)
            st64 = self.sb('st64', [64, 4, 512], BF16)
            st32 = self.sb('st32', [32, 512], BF16)
            st96 = self.sb('st96', [96, 4, 512], BF16)
            vst = self.sb('vst', [128, 14, 4, 65], BF16)
            b_w, b_tab, b_hT, b_rstd = Buf('w'), Buf('tab'), Buf('hT'), Buf('rstd')
            b_sq = [Buf('sq0'), Buf('sq1')]
            b_cqf, b_cqn, b_ckf, b_cn = Buf('cqf'), Buf('cqn'), Buf('ckf'), Buf('cn')
            b_t1 = [Buf('t1a'), Buf('t1b')]
            b_t2 = [Buf('t2a'), Buf('t2b')]
            b_st128 = [Buf('st128_%d' % i) for i in range(11)]
            b_st64 = [Buf('st64_%d' % i) for i in range(4)]
            b_st32 = Buf('st32')
            b_st96 = [Buf('st96_%d' % i) for i in range(4)]
            b_vst, b_dram = Buf('vst'), Buf('dram_out')

            gw = P.group('w')
            self.load_params(P, W, li)
            for k in range(8):
                P.dma('pool', w_fm[:, k, :], W['w_fm'][li, k * 128:(k + 1) * 128, :], gw, writes=[b_w])
            for k in range(8):
                P.dma('pool', w_tm[:, k, :], W['w_tm'][li, k * 128:(k + 1) * 128, :], gw, writes=[b_w])
            for wt, nm in ((w_uq, 'w_uq'), (w_uqs, 'w_uqs')):
                P.dma('pool', wt[:, 0, :], W[nm][li, 0:128, :], gw, writes=[b_w])
                P.dma('pool', wt[0:64, 1, :], W[nm][li, 128:192, :], gw, writes=[b_w])
            P.dma('pool', w_uk[:], W['w_uk'][li], gw, writes=[b_w])
            P.dma('pool', w_uv[:], W['w_uv'][li], gw, writes=[b_w])
            P.c('dve', 'memset', writes=[b_vst], ap=vst[:, :, :, 64:65], constant=1.0)
            gt = P.group('tab')
            gst = [P.group('st_k'), P.group('st_q'), P.group('st_v')]
            bank = [0]

            def nb():
                bank[0] = bank[0] % 7 + 1
                return bank[0]

            def proj(name):
                off, m = FM_OFF[name]
                pi = nb()
                ps = psum[0:m, pi, :]
                for k in range(8):
                    P.c('pe', 'matmul', reads=[b_w, b_hT], writes=[ps_b[pi]], acc=(k > 0),
                        out=ps, lhsT=w_fm[:, k, off:off + m], rhs=hT[:, k, :], start=(k == 0), stop=(k == 7))
                return pi, ps

            def rope_mix(pix, psx, pis, pss, tbl, m, out_ap, out_b, slot, p0=0):
                a, b = t1[slot], t2[slot]
                P.c('dve', 'tensor_tensor', reads=[ps_b[pix], b_tab], writes=[b_t1[slot]],
                    out=a[p0:p0 + m, :], in0=psx, in1=tab[p0:p0 + m, tbl, :], op=ALU.mult)
                P.c('dve', 'tensor_tensor', reads=[ps_b[pis], b_tab], writes=[b_t2[slot]],
                    out=b[p0:p0 + m, :], in0=pss, in1=tab[p0:p0 + m, tbl + 1, :], op=ALU.mult)
                P.c('pool', 'tensor_tensor', reads=[b_t1[slot], b_t2[slot]], writes=[out_b],
                    out=out_ap, in0=a[p0:p0 + m, :], in1=b[p0:p0 + m, :], op=ALU.add)

            def rope_pair(nx, ns, tbl, m, out_ap, out_b, slot):
                pix, psx = proj(nx)
                pis, pss = proj(ns)
                rope_mix(pix, psx, pis, pss, tbl, m, out_ap, out_b, slot)

            for tc in range(NTC):
                tsl = slice(tc * 512, (tc + 1) * 512)
                P.dma('sp', tab[:], rope[:, :, tsl].rearrange("a p t -> p a t"), gt, writes=[b_tab])
                srcs = [(xT[:, c, tsl], 128, [self.xT_b[c][tc]]) for c in range(8)]
                self.rstd_from(P, srcs, float(D), 0, rstd[:], b_rstd, sq, b_sq)
                for c in range(8):
                    P.c('dve', 'scalar_tensor_tensor', reads=[self.xT_b[c][tc], self.par_b, b_rstd], writes=[b_hT],
                        out=hT[:, c, :], in0=xT[:, c, tsl], scalar=par[:, P_G1 + c:P_G1 + c + 1], in1=rstd[:],
                        op0=ALU.mult, op1=ALU.mult)
                for i, nm in enumerate(('cq0', 'cq1')):
                    pi, ps = proj(nm)
                    m = FM_OFF[nm][1]
                    P.c('act', 'activation', reads=[ps_b[pi]], writes=[b_cqf], out=cqf[0:m, i, :], in_=ps, func=AF.Copy)
                pi, ps = proj('ckv')
                P.c('act', 'activation', reads=[ps_b[pi]], writes=[b_ckf], out=ckf[:], in_=ps, func=AF.Copy)
                rope_pair('kr', 'krs', 0, 32, st32[:], b_st32, 0)
                rope_pair('bq0', 'bqs0', 0, 128, st128[:, ST_QB, :], b_st128[ST_QB], 1)
                rope_pair('bq1', 'bqs1', 0, 128, st128[:, ST_QB + 1, :], b_st128[ST_QB + 1], 0)
                rope_pair('bk0', 'bks0', 0, 128, st128[:, ST_KB, :], b_st128[ST_KB], 1)
                rope_pair('bk1', 'bks1', 0, 128, st128[:, ST_KB + 1, :], b_st128[ST_KB + 1], 0)
                rope_pair('cq0_', 'cqs0', 2, 128, st128[:, ST_QC, :], b_st128[ST_QC], 1)
                rope_pair('cq1_', 'cqs1', 2, 128, st128[:, ST_QC + 1, :], b_st128[ST_QC + 1], 0)
                rope_pair('ck', 'cks', 2, 128, st128[:, ST_KC, :], b_st128[ST_KC], 1)
                for nm, idx in (('dq0', ST_QD), ('dq1', ST_QD + 1), ('dk0', ST_KD), ('dk1', ST_KD + 1)):
                    pi, ps = proj(nm)
                    P.c('act', 'activation', reads=[ps_b[pi]], writes=[b_st128[idx]], out=st128[:, idx, :], in_=ps, func=AF.Copy)
                srcs = [(cqf[:, 0, :], 128, [b_cqf]), (cqf[0:64, 1, :], 64, [b_cqf])]
                self.rstd_from(P, srcs, 192.0, 0, rstd[:], b_rstd, sq, b_sq)
                for i, m in ((0, 128), (1, 64)):
                    P.c('dve', 'scalar_tensor_tensor', reads=[b_cqf, self.par_b, b_rstd], writes=[b_cqn],
                        out=cqn[0:m, i, :], in0=cqf[0:m, i, :], scalar=par[0:m, P_GQ + i:P_GQ + i + 1], in1=rstd[0:m, :],
                        op0=ALU.mult, op1=ALU.mult)
                self.rstd_from(P, [(ckf[:], 128, [b_ckf])], 128.0, 0, rstd[:], b_rstd, sq, b_sq)
                P.c('dve', 'scalar_tensor_tensor', reads=[b_ckf, self.par_b, b_rstd], writes=[b_cn],
                    out=cn[:], in0=ckf[:], scalar=par[:, P_GKV:P_GKV + 1], in1=rstd[:], op0=ALU.mult, op1=ALU.mult)
                for h in range(4):
                    res = []
                    for wt in (w_uq, w_uqs):
                        pi = nb()
                        ps = psum[0:96, pi, :]
                        for i, m in ((0, 128), (1, 64)):
                            P.c('pe', 'matmul', reads=[b_w, b_cqn], writes=[ps_b[pi]], acc=(i > 0),
                                out=ps, lhsT=wt[0:m, i, h * 96:(h + 1) * 96], rhs=cqn[0:m, i, :], start=(i == 0), stop=(i == 1))
                        res.append((pi, ps))
                    P.c('act', 'activation', reads=[ps_b[res[0][0]]], writes=[b_st96[h]],
                        out=st96[0:64, h, :], in_=res[0][1][0:64, :], func=AF.Copy)
                    rope_mix(res[0][0], res[0][1][64:96, :], res[1][0], res[1][1][64:96, :], 0, 32,
                             st96[64:96, h, :], b_st96[h], h % 2, p0=64)
                for h in range(4):
                    pi = nb()
                    ps = psum[0:64, pi, :]
                    P.c('pe', 'matmul', reads=[b_w, b_cn], writes=[ps_b[pi]],
                        out=ps, lhsT=w_uk[:, h * 64:(h + 1) * 64], rhs=cn[:], start=True, stop=True)
                    P.c('act', 'activation', reads=[ps_b[pi]], writes=[b_st64[h]], out=st64[:, h, :], in_=ps, func=AF.Copy)
                for tt in range(4):
                    ts2 = slice(tt * 128, (tt + 1) * 128)
                    pi = nb()
                    ps = psum[:, pi, 0:256]
                    P.c('pe', 'matmul', reads=[b_w, b_cn], writes=[ps_b[pi]], out=ps, lhsT=cn[:, ts2], rhs=w_uv[:], start=True, stop=True)
                    P.c('dve', 'tensor_copy', reads=[ps_b[pi]], writes=[b_vst],
                        out=vst[:, 0:4, tt, 0:64], in_=ps.rearrange("p (s e) -> p s e", e=64))
                    pi1 = nb()
                    ps1 = psum[:, pi1, :]
                    for k in range(8):
                        P.c('pe', 'matmul', reads=[b_w, b_hT], writes=[ps_b[pi1]], acc=(k > 0),
                            out=ps1, lhsT=hT[:, k, ts2], rhs=w_tm[:, k, 0:512], start=(k == 0), stop=(k == 7))
                    P.c('act', 'activation', reads=[ps_b[pi1]], writes=[b_vst],
                        out=vst[:, 4:12, tt, 0:64], in_=ps1.rearrange("p (s e) -> p s e", e=64), func=AF.Copy)
                    pi2 = nb()
                    ps2 = psum[:, pi2, 0:128]
                    for k in range(8):
                        P.c('pe', 'matmul', reads=[b_w, b_hT], writes=[ps_b[pi2]], acc=(k > 0),
                            out=ps2, lhsT=hT[:, k, ts2], rhs=w_tm[:, k, 512:640], start=(k == 0), stop=(k == 7))
                    P.c('dve', 'tensor_copy', reads=[ps_b[pi2]], writes=[b_vst],
                        out=vst[:, 12:14, tt, 0:64], in_=ps2.rearrange("p (s e) -> p s e", e=64))
                P.dma('sp', kv_loc[0:256, tsl].rearrange("(h p) t -> p h t", p=64), st64[:], gst[0], reads=b_st64, writes=[b_dram])
                P.dma('sp', kv_loc[256:288, tsl], st32[:], gst[0], reads=[b_st32], writes=[b_dram])
                P.dma('sp', kv_loc[288:928, tsl].rearrange("(g p) t -> p g t", p=128), st128[:, 0:5, :], gst[0],
                      reads=b_st128[0:5], writes=[b_dram])
                P.dma('sp', q_loc[0:384, tsl].rearrange("(h p) t -> p h t", p=96), st96[:], gst[1], reads=b_st96, writes=[b_dram])
                P.dma('sp', q_loc[384:1152, tsl].rearrange("(g p) t -> p g t", p=128), st128[:, 5:11, :], gst[1],
                      reads=b_st128[5:11], writes=[b_dram])
                vdst = bass.AP(kv_loc.tensor, kv_loc.offset + KROWS * T + tc * 260, [[1040, 128], [65 * T, 14], [1, 260]])
                P.dma('sp', vdst, vst[:].rearrange("p s b e -> p s (b e)"), gst[2], reads=[b_vst], writes=[b_dram])
            P.emit()

    def emit_allgather(self, kv_loc, kv_all):
        nc = self.nc
        sem = self.stack.enter_context(nc.semaphore('cc%d' % getattr(self, 'ncc', 0)))
        self.ncc = getattr(self, 'ncc', 0) + 1
        with nc.Block() as block:
            @block.gpsimd
            def _(g):
                g.collective_compute("AllGather", ALU.bypass, replica_groups=[list(range(NCORES))],
                                     ins=[kv_loc[:, :]], outs=[kv_all[:, :]]).then_inc(sem, 1)
                g.wait_ge(sem, 1)

    def emit_dense(self, q_loc, kv_all, mixu, rden, passes=None):
        K = self.K
        P = Phase(K, 'dense')
        psum, ps_b = self.psum, self.ps_b
        if passes is None:
            passes = []
            for h in range(4):
                passes.append(dict(slot=h, d=96, q0=h * 96, krows=[(h * 64, 64), (256, 32)], vs=h, scale=96 ** -0.5))
            for m in range(8):
                h, c = m // 2, m % 2
                passes.append(dict(slot=4 + c * 4 + h, d=32, q0=384 + m * 32, krows=[(288 + m * 32, 32)], vs=4 + h, scale=32 ** -0.5))
        with ExitStack() as es:
            self.cur = es
            qT = [self.sb('qT%d' % i, [128, T], BF16) for i in range(2)]
            kc = [self.sb('kc%d' % i, [128, T], BF16) for i in range(2)]
            qTB = [self.sb('qTB%d' % i, [128, T], BF16) for i in range(2)]
            kcB = [self.sb('kcB%d' % i, [128, T], BF16) for i in range(2)]
            vc = [self.sb('vc%d' % i, [128, 16 * 65], BF16) for i in range(2)]
            pT = [self.sb('pT%d' % i, [128, 1024], BF16) for i in range(4)]
            osb = [self.sb('osb%d' % i, [65, T], F32) for i in range(2)]
            b_q = [Buf('q0'), Buf('q1')]
            b_k = [Buf('k0'), Buf('k1')]
            b_qB = [Buf('qB0'), Buf('qB1')]
            b_kB = [Buf('kB0'), Buf('kB1')]
            g_qB = [P.group('qB0'), P.group('qB1')]
            g_kB = [P.group('kB0'), P.group('kB1')]
            for i in range(2):
                P.c('pool', 'memset', writes=[b_qB[i]], ap=qTB[i][:, :], constant=0.0)
                P.c('pool', 'memset', writes=[b_kB[i]], ap=kcB[i][:, :], constant=0.0)
            b_v = [Buf('v0'), Buf('v1')]
            b_p = [Buf('p%d' % i) for i in range(4)]
            b_o = [Buf('o0'), Buf('o1')]
            b_dram = Buf('dram')
            g_q = [P.group('q0'), P.group('q1')]
            g_k = [P.group('k0'), P.group('k1')]
            g_v = [P.group('v0'), P.group('v1')]
            g_o = [P.group('o0'), P.group('o1')]
            ci = 0
            step = 0
            for pi_, ps_ in enumerate(passes):
                d, scale = ps_['d'], ps_['scale']
                isB = (d == 32)
                dm = 128 if isB else d
                if isB:
                    qt, bq, gq_ = qTB[pi_ % 2], b_qB[pi_ % 2], g_qB[pi_ % 2]
                else:
                    qt, bq, gq_ = qT[pi_ % 2], b_q[pi_ % 2], g_q[pi_ % 2]
                P.dma('sp', qt[0:d, :], q_loc[ps_['q0']:ps_['q0'] + d, :], gq_, writes=[bq])
                steps = [(c, kb, hf) for c in range(NCORES) for kb in range(16) for hf in range(2)]
                chunk_tiles = {}

                def load_chunk(c):
                    nonlocal ci
                    vt, bv = vc[ci % 2], b_v[ci % 2]
                    if isB:
                        kt, bk, gk_ = kcB[ci % 2], b_kB[ci % 2], g_kB[ci % 2]
                    else:
                        kt, bk, gk_ = kc[ci % 2], b_k[ci % 2], g_k[ci % 2]
                    p0 = 0
                    for (r0, n) in ps_['krows']:
                        P.dma('sp', kt[p0:p0 + n, :], kv_all[c * KVROWS + r0:c * KVROWS + r0 + n, :], gk_, writes=[bk])
                        p0 += n
                    vsrc = bass.AP(kv_all.tensor, kv_all.offset + (c * KVROWS + KROWS + 65 * ps_['vs']) * T,
                                   [[1040, 128], [1, 1040]])
                    P.dma('sp', vt[:], vsrc, g_v[ci % 2], writes=[bv])
                    chunk_tiles[c] = (kt, vt, bk, bv)
                    ci += 1

                def qk(i):
                    c, kb, hf = steps[i]
                    if c not in chunk_tiles:
                        load_chunk(c)
                    kt, vt, bk, bv = chunk_tiles[c]
                    sb0 = 4 + 2 * ((step + i) % 2)
                    for j in range(2):
                        q0 = (hf * 2 + j) * 512
                        P.c('pe', 'matmul', reads=[bk, bq], writes=[ps_b[sb0 + j]],
                            out=psum[:, sb0 + j, :], lhsT=kt[0:dm, kb * 128:(kb + 1) * 128], rhs=qt[0:dm, q0:q0 + 512],
                            start=True, stop=True)

                n = len(steps)
                qk(0)
                for i in range(n):
                    c, kb, hf = steps[i]
                    if i + 1 < n:
                        qk(i + 1)
                    kt, vt, bk, bv = chunk_tiles[c]
                    sb0 = 4 + 2 * ((step + i) % 2)
                    pt, bp = pT[(step + i) % 4], b_p[(step + i) % 4]
                    P.c('act', 'activation', reads=[ps_b[sb0], ps_b[sb0 + 1]], writes=[bp],
                        out=pt[:].rearrange("p (a b) -> p a b", a=2), in_=psum[:, sb0:sb0 + 2, :], func=AF.Exp, scale=scale)
                    first = (c == 0 and kb == 0)
                    last = (c == NCORES - 1 and kb == 15)
                    for j in range(2):
                        ob = hf * 2 + j
                        P.c('pe', 'matmul', reads=[bv, bp], writes=[ps_b[ob]], acc=(not first),
                            out=psum[0:65, ob, :], lhsT=vt[:, kb * 65:(kb + 1) * 65], rhs=pt[:, j * 512:(j + 1) * 512],
                            start=first, stop=last)
                step += n
                ot, bo = osb[pi_ % 2], b_o[pi_ % 2]
                P.c('act', 'activation', reads=ps_b[0:4], writes=[bo],
                    out=ot[:].rearrange("p (a b) -> p a b", a=4), in_=psum[0:65, 0:4, :], func=AF.Copy)
                P.c('dve', 'reciprocal', reads=[bo], writes=[bo], out=ot[64:65, :], in_=ot[64:65, :])
                s = ps_['slot']
                P.dma('pool', mixu[s * 64:(s + 1) * 64, :], ot[0:64, :], g_o[pi_ % 2], reads=[bo], writes=[b_dram])
                P.dma('pool', rden[s:s + 1, :], ot[64:65, :], g_o[pi_ % 2], reads=[bo], writes=[b_dram])
            P.emit()

    def emit_window(self, li, W, q_loc, kv_loc, kv_all, mixu, rden, ctab_d, dtab_d, sel_d, which):
        isC, isD = which == 'C', which == 'D'
        K = self.K
        P = Phase(K, 'win')
        psum, ps_b, par = self.psum, self.ps_b, self.par
        with ExitStack() as es:
            self.cur = es
            KC = self.sb('KC', [64, 2, 18 * 128], BF16) if isC else None
            VC = self.sb('VC', [128, 2, 18, 65], BF16) if isC else None
            KD = self.sb('KD', [64, 4, 20 * 128], BF16) if isD else None
            VD = self.sb('VD', [128, 4, 20, 65], BF16) if isD else None
            QC = self.sb('QC', [64, 4, T], BF16) if isC else None
            QD = self.sb('QD', [64, 4, T], BF16) if isD else None
            cK = [self.sb('cK%d' % i, [64, 4, 256], BF16) for i in range(2)]
            cV = [self.sb('cV%d' % i, [128, 4, 2, 65], BF16) for i in range(2)]
            ctab = self.sb('ctab', [128, 3, 768], F32) if isC else None
            dtab = [self.sb('dtab%d' % i, [128, 3584], F32) for i in range(2)] if isD else None
            ssb = [self.sb('ssb%d' % i, [128, 896], F32) for i in range(2)]
            pT = [self.sb('wpT%d' % i, [128, 896], BF16) for i in range(2)]
            ost = [self.sb('ost%d' % i, [65, 4, 512], F32) for i in range(2)]
            sel = self.sb('sel', [128, 16], F32)
            sk = self.sb('sk', [128, 4], F32)
            b_KC, b_VC, b_KD, b_VD, b_QC, b_QD = [Buf(n) for n in ('KC', 'VC', 'KD', 'VD', 'QC', 'QD')]
            b_cK = [Buf('cK0'), Buf('cK1')]
            b_cV = [Buf('cV0'), Buf('cV1')]
            b_ctab, b_sel, b_sk = Buf('ctab'), Buf('sel'), Buf('sk')
            b_dtab = [Buf('dt0'), Buf('dt1')]
            b_ssb = [Buf('ss0'), Buf('ss1')]
            b_pT = [Buf('wp0'), Buf('wp1')]
            b_ost = [Buf('os0'), Buf('os1')]
            b_dram = Buf('dram')
            g_own, g_tab = P.group('own'), P.group('tab')
            g_cK = [P.group('cK0'), P.group('cK1')]
            g_cV = [P.group('cV0'), P.group('cV1')]
            g_dt = P.group('dt1')
            g_os = [P.group('os0'), P.group('os1')]
            self.load_params(P, W, li)
            P.dma('sp', sel[:], sel_d, g_tab, writes=[b_sel])
            if isC:
                P.dma('sp', ctab[:], ctab_d.rearrange("c p f -> p c f"), g_tab, writes=[b_ctab])
                P.dma('sp', QC[:], q_loc[640:896, :].rearrange("(h p) t -> p h t", p=64), g_own, writes=[b_QC])
                P.dma('sp', KC[:, :, 128:128 + T], kv_loc[544:672, :].rearrange("(h p) t -> p h t", p=64), g_own, writes=[b_KC])
            if isD:
                P.dma('sp', dtab[0][:], dtab_d[li, 0], g_tab, writes=[b_dtab[0]])
                P.dma('sp', QD[:], q_loc[896:1152, :].rearrange("(h p) t -> p h t", p=64), g_own, writes=[b_QD])
                P.dma('sp', KD[:, :, 256:256 + T], kv_loc[672:928, :].rearrange("(h p) t -> p h t", p=64), g_own, writes=[b_KD])

            def vsrc(base, slot0, ns, b0, nbk):
                return bass.AP(base.tensor, base.offset + (KROWS + 65 * slot0) * T + b0 * 65,
                               [[1040, 128], [65 * T, ns], [65, nbk], [1, 65]])
            if isC:
                P.dma('sp', VC[:, :, 1:17, :], vsrc(kv_loc, 8, 2, 0, 16), g_own, writes=[b_VC])
            if isD:
                P.dma('sp', VD[:, :, 2:18, :], vsrc(kv_loc, 10, 4, 0, 16), g_own, writes=[b_VD])
            P.c('act', 'activation', reads=[self.par_b], writes=[b_sk], out=sk[:], in_=par[:, P_SINK:P_SINK + 4], func=AF.Exp)
            cnt = 0
            for side in range(2):
                for c in range(NCORES):
                    scol = side * 8 + c
                    i = cnt % 2
                    cnt += 1
                    base = kv_all[c * KVROWS:(c + 1) * KVROWS, :]
                    t0c = (T - 128) if side == 0 else 0
                    t0d = (T - 256) if side == 0 else 0
                    bC = 15 if side == 0 else 0
                    bD = 14 if side == 0 else 0
                    if isC:
                        P.dma('sp', cK[i][:, 0:2, 0:128], base[544:672, t0c:t0c + 128].rearrange("(h p) t -> p h t", p=64),
                              g_cK[i], writes=[b_cK[i]])
                        dstC = KC[:, :, 0:128] if side == 0 else KC[:, :, 17 * 128:18 * 128]
                        dVC = VC[:, :, 0:1, :] if side == 0 else VC[:, :, 17:18, :]
                    if isD:
                        dstD = KD[:, :, 0:256] if side == 0 else KD[:, :, 18 * 128:20 * 128]
                        dVD = VD[:, :, 0:2, :] if side == 0 else VD[:, :, 18:20, :]

                    def acc(dst, src, bufd, bufs, first, scol=scol):
                        sc = sel[0:dst.shape[0], scol:scol + 1]
                        if first:
                            P.c('dve', 'tensor_scalar', reads=[bufs, b_sel], writes=[bufd],
                                out=dst, in0=src, scalar1=sc, scalar2=None, op0=ALU.mult)
                        else:
                            P.c('dve', 'scalar_tensor_tensor', reads=[bufs, b_sel, bufd], writes=[bufd],
                                out=dst, in0=src, scalar=sc, in1=dst, op0=ALU.mult, op1=ALU.add)
                    if isC:
                        acc(dstC, cK[i][:, 0:2, 0:128], b_KC, b_cK[i], c == 0)
                        P.dma('sp', cV[i][:, 0:2, 0:1, :], vsrc(base, 8, 2, bC, 1), g_cV[i], writes=[b_cV[i]])
                        acc(dVC, cV[i][:, 0:2, 0:1, :], b_VC, b_cV[i], c == 0)
                    if isD:
                        P.dma('sp', cK[i][:, :, :], base[672:928, t0d:t0d + 256].rearrange("(h p) t -> p h t", p=64),
                              g_cK[i], writes=[b_cK[i]])
                        acc(dstD, cK[i][:, :, :], b_KD, b_cK[i], c == 0)
                        P.dma('sp', cV[i][:, :, :, :], vsrc(base, 10, 4, bD, 2), g_cV[i], writes=[b_cV[i]])
                        acc(dVD, cV[i][:, :, :, :], b_VD, b_cV[i], c == 0)

            sc64 = 64 ** -0.5
            wstep = [0]

            def finish(tc, oi, slot0, with_sink):
                ot, bo = ost[oi], b_ost[oi]
                if with_sink:
                    for h in range(4):
                        P.c('dve', 'tensor_scalar', reads=[bo, b_sk], writes=[bo], out=ot[64:65, h, :], in0=ot[64:65, h, :],
                            scalar1=sk[64:65, h:h + 1], scalar2=None, op0=ALU.add)
                P.c('dve', 'reciprocal', reads=[bo], writes=[bo], out=ot[64:65, :, :], in_=ot[64:65, :, :])
                tsl = slice(tc * 512, (tc + 1) * 512)
                P.dma('pool', mixu[slot0 * 64:(slot0 + 4) * 64, tsl].rearrange("(h p) t -> p h t", p=64), ot[0:64, :, :],
                      g_os[oi], reads=[bo], writes=[b_dram])
                P.dma('pool', rden[slot0:slot0 + 4, tsl].rearrange("(o h) t -> o h t", o=1), ot[64:65, :, :],
                      g_os[oi], reads=[bo], writes=[b_dram])

            oi = 0
            for ti in (range(16) if isC else []):
                cls = 1 if ti == 0 else (2 if ti == 15 else 0)
                tq = slice(ti * 128, (ti + 1) * 128)
                ob = 4 + (ti % 2)
                for kv in range(2):
                    w = wstep[0] % 2
                    wstep[0] += 1
                    sb0 = 2 * w
                    fl = flat_ps(psum, sb0, 2)
                    for j in range(3):
                        P.c('pe', 'matmul', reads=[b_KC, b_QC], writes=[ps_b[sb0], ps_b[sb0 + 1]],
                            out=fl[:, j * 256:(j + 1) * 256].rearrange("p (g q) -> p g q", g=2),
                            lhsT=KC[:, kv, (ti + j) * 128:(ti + j + 1) * 128], rhs=QC[:, 2 * kv:2 * kv + 2, tq],
                            start=True, stop=True)
                    P.c('dve', 'scalar_tensor_tensor', reads=[ps_b[sb0], ps_b[sb0 + 1], b_ctab], writes=[b_ssb[w]],
                        out=ssb[w][:, 0:768], in0=fl[:, 0:768], scalar=sc64, in1=ctab[:, cls, :], op0=ALU.mult, op1=ALU.add)
                    P.c('act', 'activation', reads=[b_ssb[w]], writes=[b_pT[w]], out=pT[w][:, 0:768], in_=ssb[w][:, 0:768], func=AF.Exp)
                    for g in range(2):
                        h = 2 * kv + g
                        for j in range(3):
                            P.c('pe', 'matmul', reads=[b_VC, b_pT[w]], writes=[ps_b[ob]], acc=(j > 0),
                                out=psum[0:65, ob, h * 128:(h + 1) * 128], lhsT=VC[:, kv, ti + j, :],
                                rhs=pT[w][:, j * 256 + g * 128:j * 256 + (g + 1) * 128], start=(j == 0), stop=(j == 2))
                tl = ti % 4
                P.c('act', 'activation', reads=[ps_b[ob]], writes=[b_ost[oi]],
                    out=ost[oi][0:65, :, tl * 128:(tl + 1) * 128], in_=psum[0:65, ob, :].rearrange("p (h q) -> p h q", h=4), func=AF.Copy)
                if tl == 3:
                    finish(ti // 4, oi, 12, True)
                    oi = 1 - oi
            for ti in (range(16) if isD else []):
                special = {0: 1, 1: 2, 14: 3, 15: 4}.get(ti)
                if special is not None:
                    P.dma('sp', dtab[1][:], dtab_d[li, special], g_dt, writes=[b_dtab[1]])
                    tb, btb = dtab[1], b_dtab[1]
                else:
                    tb, btb = dtab[0], b_dtab[0]
                j0, j1 = (1, 6) if ti < 2 else ((0, 5) if ti >= 14 else (1, 5))
                nj = j1 - j0 + 1
                tq = slice(ti * 128, (ti + 1) * 128)
                ob = 4 + (ti % 2)
                for h in range(4):
                    w = wstep[0] % 2
                    wstep[0] += 1
                    sb0 = 2 * w
                    fl = flat_ps(psum, sb0, 2)
                    for j in range(j0, j1 + 1):
                        blk = ti - 1 + j
                        P.c('pe', 'matmul', reads=[b_KD, b_QD], writes=[ps_b[sb0], ps_b[sb0 + 1]],
                            out=fl[:, (j - j0) * 128:(j - j0 + 1) * 128], lhsT=KD[:, h, blk * 128:(blk + 1) * 128], rhs=QD[:, h, tq],
                            start=True, stop=True)
                    P.c('dve', 'scalar_tensor_tensor', reads=[ps_b[sb0], ps_b[sb0 + 1], btb], writes=[b_ssb[w]],
                        out=ssb[w][:, 0:nj * 128], in0=fl[:, 0:nj * 128], scalar=sc64,
                        in1=tb[:, (h * 7 + j0) * 128:(h * 7 + j0 + nj) * 128], op0=ALU.mult, op1=ALU.add)
                    P.c('act', 'activation', reads=[b_ssb[w]], writes=[b_pT[w]], out=pT[w][:, 0:nj * 128], in_=ssb[w][:, 0:nj * 128], func=AF.Exp)
                    for j in range(j0, j1 + 1):
                        blk = ti - 1 + j
                        P.c('pe', 'matmul', reads=[b_VD, b_pT[w]], writes=[ps_b[ob]], acc=(j > j0),
                            out=psum[0:65, ob, h * 128:(h + 1) * 128], lhsT=VD[:, h, blk, :],
                            rhs=pT[w][:, (j - j0) * 128:(j - j0 + 1) * 128], start=(j == j0), stop=(j == j1))
                tl = ti % 4
                P.c('act', 'activation', reads=[ps_b[ob]], writes=[b_ost[oi]],
                    out=ost[oi][0:65, :, tl * 128:(tl + 1) * 128], in_=psum[0:65, ob, :].rearrange("p (h q) -> p h q", h=4), func=AF.Copy)
                if tl == 3:
                    finish(ti // 4, oi, 16, False)
                    oi = 1 - oi
            P.emit()

    def emit_mix(self, li, lam_init, W, mixu, rden):
        K = self.K
        P = Phase(K, 'mix')
        psum, ps_b, par, xT = self.psum, self.ps_b, self.par, self.xT
        with ExitStack() as es:
            self.cur = es
            num = self.sb('num', [128, 10, 512], F32)
            rdn = self.sb('rdn', [128, 10, 512], F32)
            yb = self.sb('yb', [128, 2, 512], F32)
            mixn = self.sb('mixn', [128, 8, 512], BF16)
            w_out = self.sb('w_out', [128, 8, D], BF16)
            sq = [self.sb('msq%d' % i, [128, 512], F32) for i in range(2)]
            rstd = self.sb('mrstd', [128, 512], F32)
            lt = self.sb('lt', [128, 40], F32)
            b_num, b_rdn, b_yb, b_mixn, b_w, b_rstd, b_lt = [Buf(n) for n in ('num', 'rdn', 'yb', 'mixn', 'w', 'rstd', 'lt')]
            b_sq = [Buf('sq0'), Buf('sq1')]
            gw, gn, gr = P.group('w'), P.group('num'), P.group('rdn')
            self.load_params(P, W, li)
            for k in range(8):
                P.dma('pool', w_out[:, k, :], W['w_out'][li, k * 128:(k + 1) * 128, :], gw, writes=[b_w])
            for i in range(2):
                P.c('dve', 'tensor_tensor', reads=[self.par_b], writes=[b_lt], out=lt[:, 0:32],
                    in0=par[:, P_LAM + 64 * i:P_LAM + 64 * i + 32], in1=par[:, P_LAM + 64 * i + 32:P_LAM + 64 * i + 64], op=ALU.mult)
                P.c('dve', 'tensor_reduce', reads=[b_lt], writes=[b_lt], out=lt[:, 32 + i:33 + i], in_=lt[:, 0:32],
                    axis=mybir.AxisListType.X, op=ALU.add)
            P.c('act', 'activation', reads=[b_lt], writes=[b_lt], out=lt[:, 34:36], in_=lt[:, 32:34], func=AF.Exp)
            P.c('dve', 'tensor_tensor', reads=[b_lt], writes=[b_lt], out=lt[:, 36:37], in0=lt[:, 35:36], in1=lt[:, 34:35], op=ALU.subtract)
            P.c('dve', 'tensor_tensor', reads=[b_lt, self.par_b], writes=[b_lt], out=lt[:, 37:38], in0=lt[:, 36:37],
                in1=par[:, P_LI:P_LI + 1], op=ALU.subtract)
            P.c('dve', 'tensor_tensor', reads=[b_lt, self.par_b], writes=[b_lt], out=lt[:, 38:39], in0=par[:, P_GSUB:P_GSUB + 1],
                in1=par[:, P_OML:P_OML + 1], op=ALU.mult)
            bank = [0]

            def nb():
                bank[0] = bank[0] % 7 + 1
                return bank[0]
            for tc in range(NTC):
                tsl = slice(tc * 512, (tc + 1) * 512)
                P.dma('sp', num[:], mixu[:, tsl].rearrange("(q p) t -> p q t", p=128), gn, writes=[b_num])
                for s in range(NSLOT):
                    src = bass.AP(rden.tensor, rden.offset + s * T + tc * 512, [[0, 64], [1, 512]])
                    P.dma('sp', rdn[(s % 2) * 64:(s % 2) * 64 + 64, s // 2, :], src, gr, writes=[b_rdn])
                P.c('dve', 'tensor_tensor', reads=[b_num, b_rdn], writes=[b_num], out=num[:], in0=num[:], in1=rdn[:], op=ALU.mult)
                for i in range(2):
                    P.c('dve', 'scalar_tensor_tensor', reads=[b_num, b_lt], writes=[b_yb], out=yb[:, i, :], in0=num[:, 4 + i, :],
                        scalar=lt[:, 37:38], in1=num[:, 2 + i, :], op0=ALU.mult, op1=ALU.add)

                def gnorm(chunks, gcol, mo):
                    srcs = [(ap, 128, [bb]) for ap, bb in chunks]
                    self.rstd_from(P, srcs, 256.0, 0, rstd[:], b_rstd, sq, b_sq)
                    for i, (ap, bb) in enumerate(chunks):
                        P.c('dve', 'scalar_tensor_tensor', reads=[bb, self.par_b, b_rstd], writes=[b_mixn], out=mixn[:, mo + i, :],
                            in0=ap, scalar=par[:, gcol + i:gcol + i + 1], in1=rstd[:], op0=ALU.mult, op1=ALU.mult)
                gnorm([(num[:, 0, :], b_num), (num[:, 1, :], b_num)], P_GA, 0)
                for i in range(2):
                    self.rstd_from(P, [(yb[:, i, :], 128, [b_yb])], 64.0, 0, rstd[:], b_rstd, sq, b_sq, lhs=self.bd_f)
                    P.c('dve', 'scalar_tensor_tensor', reads=[b_yb, b_lt, b_rstd], writes=[b_mixn], out=mixn[:, 2 + i, :],
                        in0=yb[:, i, :], scalar=lt[:, 38:39], in1=rstd[:], op0=ALU.mult, op1=ALU.mult)
                gnorm([(num[:, 6, :], b_num), (num[:, 7, :], b_num)], P_GC, 4)
                gnorm([(num[:, 8, :], b_num), (num[:, 9, :], b_num)], P_GD, 6)
                for oc in range(8):
                    pi = nb()
                    for k in range(8):
                        P.c('pe', 'matmul', reads=[b_w, b_mixn], writes=[ps_b[pi]], acc=(k > 0), out=psum[:, pi, :],
                            lhsT=w_out[:, k, oc * 128:(oc + 1) * 128], rhs=mixn[:, k, :], start=(k == 0), stop=(k == 7))
                    P.c('dve', 'tensor_tensor', reads=[ps_b[pi], self.xT_b[oc][tc]], writes=[self.xT_b[oc][tc]],
                        out=xT[:, oc, tsl], in0=xT[:, oc, tsl], in1=psum[:, pi, :], op=ALU.add)
            P.emit()

    def emit_ffn(self, li, W):
        K = self.K
        P = Phase(K, 'ffn')
        psum, ps_b, par, xT = self.psum, self.ps_b, self.par, self.xT
        with ExitStack() as es:
            self.cur = es
            h2 = self.sb('h2', [128, 8, T], BF16)
            wu = [self.sb('wu%d' % i, [128, 8, 1024], BF16) for i in range(2)]
            wd = [self.sb('wd%d' % i, [128, 8, D], BF16) for i in range(2)]
            aT = [self.sb('aT%d' % i, [128, 8, 512], BF16) for i in range(2)]
            rl = [self.sb('rl%d' % i, [128, 512], F32) for i in range(2)]
            sq = [self.sb('fsq%d' % i, [128, 512], F32) for i in range(2)]
            rstd = self.sb('frstd', [128, 512], F32)
            b_h2 = [Buf('h2_%d' % i) for i in range(NTC)]
            b_wu, b_wd = [Buf('wu0'), Buf('wu1')], [Buf('wd0'), Buf('wd1')]
            b_aT = [[Buf('aT%d_%d' % (i, f)) for f in range(8)] for i in range(2)]
            b_rl = [Buf('rl0'), Buf('rl1')]
            b_sq = [Buf('sq0'), Buf('sq1')]
            b_rstd = Buf('rstd')
            g_wu, g_wd = [P.group('wu0'), P.group('wu1')], [P.group('wd0'), P.group('wd1')]
            self.load_params(P, W, li)
            for tc in range(NTC):
                tsl = slice(tc * 512, (tc + 1) * 512)
                srcs = [(xT[:, c, tsl], 128, [self.xT_b[c][tc]]) for c in range(8)]
                self.rstd_from(P, srcs, float(D), 0, rstd[:], b_rstd, sq, b_sq)
                for c in range(8):
                    P.c('dve', 'scalar_tensor_tensor', reads=[self.xT_b[c][tc], self.par_b, b_rstd], writes=[b_h2[tc]],
                        out=h2[:, c, tsl], in0=xT[:, c, tsl], scalar=par[:, P_G2 + c:P_G2 + c + 1], in1=rstd[:],
                        op0=ALU.mult, op1=ALU.mult)
            ub, db, st = 0, 0, 0
            for fq in range(4):
                w = fq % 2
                for k in range(8):
                    P.dma('pool', wu[w][:, k, :], W['w_up'][li, k * 128:(k + 1) * 128, fq * 1024:(fq + 1) * 1024], g_wu[w], writes=[b_wu[w]])
                for k in range(8):
                    P.dma('pool', wd[w][:, k, :], W['w_down'][li, fq * 1024 + k * 128:fq * 1024 + (k + 1) * 128, :], g_wd[w], writes=[b_wd[w]])
                for tc in range(NTC):
                    tsl = slice(tc * 512, (tc + 1) * 512)
                    a = st % 2
                    st += 1
                    for fc in range(8):
                        pi = 1 + ub % 3
                        ub += 1
                        for k in range(8):
                            P.c('pe', 'matmul', reads=[b_wu[w], b_h2[tc]], writes=[ps_b[pi]], acc=(k > 0), out=psum[:, pi, :],
                                lhsT=wu[w][:, k, fc * 128:(fc + 1) * 128], rhs=h2[:, k, tsl], start=(k == 0), stop=(k == 7))
                        r = (ub) % 2
                        P.c('act', 'activation', reads=[ps_b[pi]], writes=[b_rl[r]], out=rl[r][:], in_=psum[:, pi, :], func=AF.Relu)
                        P.c('pool', 'tensor_tensor', reads=[b_rl[r]], writes=[b_aT[a][fc]], out=aT[a][:, fc, :], in0=rl[r][:], in1=rl[r][:], op=ALU.mult)
                    for oc in range(8):
                        pi = 4 + db % 4
                        db += 1
                        for k in range(8):
                            P.c('pe', 'matmul', reads=[b_wd[w], b_aT[a][k]], writes=[ps_b[pi]], acc=(k > 0), out=psum[:, pi, :],
                                lhsT=wd[w][:, k, oc * 128:(oc + 1) * 128], rhs=aT[a][:, k, :], start=(k == 0), stop=(k == 7))
                        P.c('dve', 'tensor_tensor', reads=[ps_b[pi], self.xT_b[oc][tc]], writes=[self.xT_b[oc][tc]],
                            out=xT[:, oc, tsl], in0=xT[:, oc, tsl], in1=psum[:, pi, :], op=ALU.add)
            P.emit()

    def emit_final(self, gfin_d, out_d, xout_d=None):
        K = self.K
        P = Phase(K, 'fin')
        xT = self.xT
        with ExitStack() as es:
            self.cur = es
            gf = self.sb('gf', [128, 8], F32)
            yo = [self.sb('yo%d' % i, [128, 8, 512], F32) for i in range(2)]
            sq = [self.sb('nsq%d' % i, [128, 512], F32) for i in range(2)]
            rstd = self.sb('nrstd', [128, 512], F32)
            b_gf, b_rstd = Buf('gf'), Buf('rstd')
            b_yo = [Buf('yo0'), Buf('yo1')]
            b_sq = [Buf('sq0'), Buf('sq1')]
            b_dram = Buf('dram')
            gg = P.group('gf')
            gy = [P.group('yo0'), P.group('yo1')]
            gx = P.group('xo')
            P.dma('sp', gf[:], gfin_d, gg, writes=[b_gf])
            if xout_d is not None:
                for c in range(8):
                    P.dma('pool', xout_d[c * 128:(c + 1) * 128, :], xT[:, c, :], gx, reads=list(self.xT_b[c]), writes=[b_dram])
            for tc in range(NTC):
                tsl = slice(tc * 512, (tc + 1) * 512)
                srcs = [(xT[:, c, tsl], 128, [self.xT_b[c][tc]]) for c in range(8)]
                self.rstd_from(P, srcs, float(D), 0, rstd[:], b_rstd, sq, b_sq)
                y = yo[tc % 2]
                for c in range(8):
                    P.c('dve', 'scalar_tensor_tensor', reads=[self.xT_b[c][tc], b_gf, b_rstd], writes=[b_yo[tc % 2]],
                        out=y[:, c, :], in0=xT[:, c, tsl], scalar=gf[:, c:c + 1], in1=rstd[:], op0=ALU.mult, op1=ALU.mult)
                P.dma('sp', out_d[:, tsl].rearrange("(c p) t -> p c t", p=128), y[:], gy[tc % 2], reads=[b_yo[tc % 2]], writes=[b_dram])
            P.emit()


W1_SHAPES = dict(w_fm=[D, NFM], w_tm=[D, NTM], w_uq=[192, 384], w_uqs=[192, 384], w_uk=[128, 256], w_uv=[128, 256],
                 params=[128, NPAR_L])
W2_SHAPES = dict(w_out=[D, D], w_up=[D, DFF], w_down=[DFF, D], params=[128, NPAR_L])


def build_l1():
    nc = bass.Bass("TRN2", target_bir_lowering=False)
    st = ExitStack()
    B = Builder(nc, st)
    W = {k: nc.dram_tensor(k, [1] + s, F32, kind="ExternalInput").ap() for k, s in W1_SHAPES.items()}
    xT_d = nc.dram_tensor("xT_in", [D, T], F32, kind="ExternalInput").ap()
    rope = nc.dram_tensor("rope", [4, 128, T], F32, kind="ExternalInput").ap()
    q_loc = nc.dram_tensor("q_loc", [QROWS, T], BF16, kind="ExternalOutput").ap()
    kv_loc = nc.dram_tensor("kv_loc", [KVROWS, T], BF16, kind="ExternalOutput").ap()
    B.setup_persistent()
    B.emit_init(xT_d)
    B.emit_phase1(0, W, q_loc, kv_loc, rope)
    return nc, st


def build_l2(lam_init, final):
    nc = bass.Bass("TRN2", target_bir_lowering=False)
    st = ExitStack()
    B = Builder(nc, st)
    W = {k: nc.dram_tensor(k, [1] + s, F32, kind="ExternalInput").ap() for k, s in W2_SHAPES.items()}
    xT_d = nc.dram_tensor("xT_in", [D, T], F32, kind="ExternalInput").ap()
    q_loc = nc.dram_tensor("q_loc", [QROWS, T], BF16, kind="ExternalInput").ap()
    kv_loc = nc.dram_tensor("kv_loc", [KVROWS, T], BF16, kind="ExternalInput").ap()
    kv_all = nc.dram_tensor("kv_all", [NCORES * KVROWS, T], BF16, kind="ExternalInput").ap()
    ctab = nc.dram_tensor("ctab", [3, 128, 768], F32, kind="ExternalInput").ap()
    dtab = nc.dram_tensor("dtab", [1, 5, 128, 3584], F32, kind="ExternalInput").ap()
    sel = nc.dram_tensor("sel", [128, 16], F32, kind="ExternalInput").ap()
    gfin = nc.dram_tensor("gfin", [128, 8], F32, kind="ExternalInput").ap()
    mixu = nc.dram_tensor("mixu", [NSLOT * 64, T], F32, kind="Internal").ap()
    rden = nc.dram_tensor("rden", [NSLOT, T], F32, kind="Internal").ap()
    xout = nc.dram_tensor("xT_out", [D, T], F32, kind="ExternalOutput").ap()
    yout = nc.dram_tensor("yT_out", [D, T], F32, kind="ExternalOutput").ap()
    B.setup_persistent()
    B.emit_init(xT_d)
    B.emit_dense(q_loc, kv_all, mixu, rden)
    B.emit_window(0, W, q_loc, kv_loc, kv_all, mixu, rden, ctab, dtab, sel, 'C')
    B.emit_window(0, W, q_loc, kv_loc, kv_all, mixu, rden, ctab, dtab, sel, 'D')
    B.emit_mix(0, lam_init, W, mixu, rden)
    B.emit_ffn(0, W)
    B.emit_final(gfin, yout, xout)
    return nc, st


WF_SHAPES = dict(w_fm=[D, NFM], w_tm=[D, NTM], w_uq=[192, 384], w_uqs=[192, 384], w_uk=[128, 256], w_uv=[128, 256],
                 params=[128, NPAR_L], w_out=[D, D], w_up=[D, DFF], w_down=[DFF, D])


def build_fused():
    nc = bass.Bass("TRN2", target_bir_lowering=False)
    st = ExitStack()
    B = Builder(nc, st)
    W = {k: nc.dram_tensor(k, [L] + s, F32, kind="ExternalInput").ap() for k, s in WF_SHAPES.items()}
    xT_d = nc.dram_tensor("xT_in", [D, T], F32, kind="ExternalInput").ap()
    rope = nc.dram_tensor("rope", [4, 128, T], F32, kind="ExternalInput").ap()
    ctab = nc.dram_tensor("ctab", [3, 128, 768], F32, kind="ExternalInput").ap()
    dtab = nc.dram_tensor("dtab", [L, 5, 128, 3584], F32, kind="ExternalInput").ap()
    sel = nc.dram_tensor("sel", [128, 16], F32, kind="ExternalInput").ap()
    gfin = nc.dram_tensor("gfin", [128, 8], F32, kind="ExternalInput").ap()
    q_loc = nc.dram_tensor("q_loc", [QROWS, T], BF16, kind="Internal").ap()
    kv_loc = nc.dram_tensor("kv_loc", [KVROWS, T], BF16, kind="Internal").ap()
    kv_all = [nc.dram_tensor("kv_all%d" % i, [NCORES * KVROWS, T], BF16, kind="Internal").ap() for i in range(2)]
    mixu = nc.dram_tensor("mixu", [NSLOT * 64, T], F32, kind="Internal").ap()
    rden = nc.dram_tensor("rden", [NSLOT, T], F32, kind="Internal").ap()
    yout = nc.dram_tensor("yT_out", [D, T], F32, kind="ExternalOutput").ap()
    B.setup_persistent()
    B.emit_init(xT_d)
    for l in range(L):
        ka = kv_all[l % 2]
        B.emit_phase1(l, W, q_loc, kv_loc, rope)
        B.emit_allgather(kv_loc, ka)
        B.emit_dense(q_loc, ka, mixu, rden)
        B.emit_window(l, W, q_loc, kv_loc, ka, mixu, rden, ctab, dtab, sel, 'C')
        B.emit_window(l, W, q_loc, kv_loc, ka, mixu, rden, ctab, dtab, sel, 'D')
        B.emit_mix(l, 0.0, W, mixu, rden)
        B.emit_ffn(l, W)
    B.emit_final(gfin, yout, None)
    return nc, st


def kernel(**inp):
    inp = {k: np.asarray(v) for k, v in inp.items()}
    x = inp['x'][0]
    cores = list(range(NCORES))
    sh = _prep_shared(inp, list(range(L)))
    in_maps = []
    for r in cores:
        m = {k: sh[k] for k in WF_SHAPES}
        m['xT_in'] = np.ascontiguousarray(x[r * T:(r + 1) * T].T)
        m['rope'] = _rope_tables(r)
        m['ctab'] = _c_tables(r)
        m['dtab'] = _d_tables(inp['na_rpb'], r, list(range(L)))
        m['sel'] = _prep_core_sel(r)
        m['gfin'] = sh['gfin']
        in_maps.append(m)
    nc, st = build_fused()
    res = run_bass_kernel_spmd(nc, in_maps, core_ids=cores).results
    st.close()
    out = np.concatenate([np.asarray(res[r]['yT_out']).T for r in cores], axis=0)[None].astype(np.float32)
    return out


def kernel_unfused(**inp):
    inp = {k: np.asarray(v) for k, v in inp.items()}
    x = inp['x'][0]
    cores = list(range(NCORES))
    xT = [np.ascontiguousarray(x[r * T:(r + 1) * T].T) for r in cores]
    ropes = [_rope_tables(r) for r in cores]
    ctabs = [_c_tables(r) for r in cores]
    sels = [_prep_core_sel(r) for r in cores]
    y = None
    nc1, st1 = build_l1()
    nc2, st2 = build_l2(0.0, True)
    for l in range(L):
        sh = _prep_shared(inp, [l])
        in1 = []
        for r in cores:
            m = {k: sh[k] for k in W1_SHAPES}
            m['xT_in'] = xT[r]
            m['rope'] = ropes[r]
            in1.append(m)
        r1 = run_bass_kernel_spmd(nc1, in1, core_ids=cores).results
        kv_all = np.ascontiguousarray(np.concatenate([r1[r]['kv_loc'] for r in cores], axis=0))
        in2 = []
        for r in cores:
            m = {k: sh[k] for k in W2_SHAPES}
            m['xT_in'] = xT[r]
            m['q_loc'] = r1[r]['q_loc']
            m['kv_loc'] = r1[r]['kv_loc']
            m['kv_all'] = kv_all
            m['ctab'] = ctabs[r]
            m['dtab'] = _d_tables(inp['na_rpb'], r, [l])
            m['sel'] = sels[r]
            m['gfin'] = sh['gfin']
            in2.append(m)
        r2 = run_bass_kernel_spmd(nc2, in2, core_ids=cores).results
        xT = [np.asarray(r2[r]['xT_out']) for r in cores]
        y = [np.asarray(r2[r]['yT_out']) for r in cores]
    st1.close()
    st2.close()
    out = np.concatenate([yy.T for yy in y], axis=0)[None].astype(np.float32)
    return out


def _prep_core_sel(rank):
    sel = np.zeros((128, 16), np.float32)
    if rank > 0:
        sel[:, rank - 1] = 1.0
    if rank < NCORES - 1:
        sel[:, 8 + rank + 1] = 1.0
    return sel
```
